# Optimizing a Trainium2 kernel written in Bass

```python
import jax, jax.numpy as jnp
from jax import lax
import numpy as np

D_MODEL = 1024
BATCH = 16
SEQ = 256
DEPTH = 1
DEC_BATCH = 4
DEC_SEQ = 4096
PAST_LEN = 512

GRID_W = 64
D_INNER = 2 * D_MODEL
D_SSD = D_INNER // 2
D_CM = D_INNER - D_SSD
SSD_HEAD_DIM = 64
N_SSD_HEADS = D_SSD // SSD_HEAD_DIM
N_SSD_GROUPS = 4
D_STATE = 128
D_CONV = 5
SSD_CHUNK = 128
CM_CHUNK = 128
N_CM_HEADS = 8
CM_HEAD_DIM = D_CM // N_CM_HEADS
D_FF = 4 * D_MODEL
N_MOD = 6
EPS = 1e-6
D_XBC = D_SSD + 2 * N_SSD_GROUPS * D_STATE
D_IN_PROJ = D_SSD + D_XBC + 2 * N_SSD_HEADS + 2 * D_CM
SPLITS = (D_SSD, D_SSD + D_XBC, D_SSD + D_XBC + 2 * N_SSD_HEADS, D_SSD + D_XBC + 2 * N_SSD_HEADS + D_CM)

kernel_name = "hybrid_ssd_chunkmlp_diffusion_step"


def rmsnorm(x, g):
    xf = x.astype(jnp.float32)
    y = xf * lax.rsqrt(jnp.mean(xf * xf, axis=-1, keepdims=True) + EPS)
    return (y * g.astype(jnp.float32)).astype(x.dtype)


def layernorm(x, g, b):
    xf = x.astype(jnp.float32)
    mu = jnp.mean(xf, axis=-1, keepdims=True)
    var = jnp.mean(jnp.square(xf - mu), axis=-1, keepdims=True)
    y = (xf - mu) * lax.rsqrt(var + EPS)
    return (y * g.astype(jnp.float32) + b.astype(jnp.float32)).astype(x.dtype)


def sincos_2d(L, dtype):
    rows = L // GRID_W
    r, col = jnp.meshgrid(jnp.arange(rows, dtype=jnp.float32), jnp.arange(GRID_W, dtype=jnp.float32), indexing='ij')
    r, col = r.reshape(L), col.reshape(L)
    nf = D_MODEL // 4
    omega = 1.0 / (10000.0 ** (jnp.arange(nf, dtype=jnp.float32) / nf))
    ar = r[:, None] * omega
    ac = col[:, None] * omega
    return jnp.concatenate([jnp.sin(ar), jnp.cos(ar), jnp.sin(ac), jnp.cos(ac)], axis=-1).astype(dtype)


def centred_dwconv(x, w, b):
    pad = D_CONV // 2
    L = x.shape[1]
    xp = jnp.pad(x, ((0, 0), (pad, pad), (0, 0)))
    out = b
    for k in range(D_CONV):
        out = out + xp[:, k:k + L] * w[k]
    return out


def ssd_chunked(x, dt, A, Bm, Cm, h0):
    out_dtype = x.dtype
    b, L, H, P = x.shape
    G, N = Bm.shape[2], Bm.shape[3]
    hg = H // G
    Q = SSD_CHUNK
    nc = L // Q
    xc = x.astype(jnp.float32).reshape(b, nc, Q, G, hg, P)
    dtc = dt.reshape(b, nc, Q, G, hg)
    Bc = Bm.astype(jnp.float32).reshape(b, nc, Q, G, N)
    Cc = Cm.astype(jnp.float32).reshape(b, nc, Q, G, N)
    a_cum = jnp.cumsum(dtc * A.reshape(G, hg), axis=2)
    xdt = xc * dtc[..., None]
    seg = a_cum[:, :, :, None] - a_cum[:, :, None, :]
    mask = jnp.tril(jnp.ones((Q, Q), dtype=bool))[None, None, :, :, None, None]
    decay = jnp.exp(jnp.where(mask, seg, -jnp.inf))
    cb = jnp.einsum('bcign,bcjgn->bcijg', Cc, Bc)
    y_diag = jnp.einsum('bcijgh,bcjghp->bcighp', cb[..., None] * decay, xdt)
    decay_to_end = jnp.exp(a_cum[:, :, -1:] - a_cum)
    states = jnp.einsum('bcjgn,bcjghp->bcghpn', Bc, decay_to_end[..., None] * xdt)
    chunk_decay = jnp.exp(a_cum[:, :, -1])

    def step(h, inp):
        s, dA = inp
        return h * dA[..., None, None] + s, h

    h_init = h0.astype(jnp.float32).reshape(b, G, hg, P, N)
    h_final, h_starts = lax.scan(step, h_init, (jnp.moveaxis(states, 1, 0), jnp.moveaxis(chunk_decay, 1, 0)))
    h_starts = jnp.moveaxis(h_starts, 0, 1)
    y_off = jnp.einsum('bcign,bcghpn->bcighp', Cc, h_starts) * jnp.exp(a_cum)[..., None]
    y = (y_diag + y_off).reshape(b, L, H, P)
    return y.astype(out_dtype), h_final.reshape(b, H, P, N).astype(out_dtype)


def ssd_mixer(z, xbc, dt_raw, conv_w, conv_b, dt_bias, A_log, d_skip, norm_g, h0):
    b, L, _ = z.shape
    xbc = jax.nn.silu(centred_dwconv(xbc, conv_w, conv_b))
    xs, Bm, Cm = jnp.split(xbc, (D_SSD, D_SSD + N_SSD_GROUPS * D_STATE), axis=-1)
    xh = xs.reshape(b, L, N_SSD_HEADS, SSD_HEAD_DIM)
    Bm = Bm.reshape(b, L, N_SSD_GROUPS, D_STATE)
    Cm = Cm.reshape(b, L, N_SSD_GROUPS, D_STATE)
    dt = jax.nn.softplus(dt_raw.astype(jnp.float32) + dt_bias.astype(jnp.float32))
    A = -jnp.exp(A_log.astype(jnp.float32))
    flip = lambda t: jnp.flip(t, axis=1)
    y_f, hf_f = ssd_chunked(xh, dt[:, :, 0], A[0], Bm, Cm, h0[:, 0])
    y_b, hf_b = ssd_chunked(flip(xh), flip(dt[:, :, 1]), A[1], flip(Bm), flip(Cm), h0[:, 1])
    y = y_f + flip(y_b) + xh * d_skip[:, None]
    y = y.reshape(b, L, D_SSD)
    out = rmsnorm(y * jax.nn.silu(z), norm_g)
    return out, jnp.stack([hf_f, hf_b], axis=1)


def chunk_mlp(u, v, ln_g, ln_b, w_s, b_s):
    u = jax.nn.gelu(u)
    v = layernorm(jax.nn.gelu(v), ln_g, ln_b)
    b, L, _ = v.shape
    nc = L // CM_CHUNK
    vc = v.reshape(b, nc, CM_CHUNK, N_CM_HEADS, CM_HEAD_DIM)
    mixed = jnp.einsum('hts,bcshd->bcthd', w_s, vc) + b_s.T[:, :, None]
    return u * mixed.reshape(b, L, D_CM)


def trunk_layer(x, cond, h0, w_ada, b_ada, norm1_g, w_in, conv_w, conv_b, dt_bias, A_log, d_skip,
                ssd_norm_g, cm_ln_g, cm_ln_b, cm_w_s, cm_b_s, w_out, norm2_g, w_ff1, w_ff2):
    b, L, _ = x.shape
    mod = jax.nn.silu(cond) @ w_ada + b_ada
    sh1, sc1, g1, sh2, sc2, g2 = [m[:, None, :] for m in jnp.split(mod, N_MOD, axis=-1)]
    h = rmsnorm(x, norm1_g) * (1 + sc1) + sh1
    proj = h @ w_in
    z, xbc, dt_raw, u, v = jnp.split(proj, SPLITS, axis=-1)
    y_ssd, h_fin = ssd_mixer(z, xbc, dt_raw.reshape(b, L, 2, N_SSD_HEADS), conv_w, conv_b,
                             dt_bias, A_log, d_skip, ssd_norm_g, h0)
    y_cm = chunk_mlp(u, v, cm_ln_g, cm_ln_b, cm_w_s, cm_b_s)
    x = x + g1 * (jnp.concatenate([y_ssd, y_cm], axis=-1) @ w_out)
    h = rmsnorm(x, norm2_g) * (1 + sc2) + sh2
    x = x + g2 * (jnp.square(jax.nn.relu(h @ w_ff1)) @ w_ff2)
    return x, h_fin


def setup_inputs(seed: int = 0) -> dict:
    key = jax.random.key(seed)
    ks = jax.random.split(key, 32)
    nrm = lambda k, shape, s: jax.random.normal(k, shape, jnp.float32) * s
    dt0 = jnp.exp(jax.random.uniform(ks[10], (DEPTH, 2, N_SSD_HEADS), jnp.float32,
                                     np.log(1e-3).astype(np.float32), np.log(1e-1).astype(np.float32)))
    return {
        "x_prompt": nrm(ks[0], (BATCH, SEQ, D_MODEL), 1.0),
        "x_sample": nrm(ks[1], (DEC_BATCH, DEC_SEQ, D_MODEL), 1.0),
        "state_ssd": nrm(ks[2], (DEC_BATCH, DEPTH, 2, N_SSD_HEADS, SSD_HEAD_DIM, D_STATE), 0.1),
        "c": nrm(ks[3], (DEC_BATCH, D_MODEL), 1.0),
        "c_ctx": nrm(ks[4], (D_MODEL,), 1.0),
        "w_ada": nrm(ks[5], (DEPTH, D_MODEL, N_MOD * D_MODEL), D_MODEL ** -0.5),
        "b_ada": nrm(ks[6], (DEPTH, N_MOD * D_MODEL), 0.02),
        "norm1_g": 1.0 + nrm(ks[7], (DEPTH, D_MODEL), 0.02),
        "w_in": nrm(ks[8], (DEPTH, D_MODEL, D_IN_PROJ), D_MODEL ** -0.5),
        "conv_w": nrm(ks[9], (DEPTH, D_CONV, D_XBC), D_CONV ** -0.5),
        "conv_b": nrm(ks[11], (DEPTH, D_XBC), 0.02),
        "dt_bias": dt0 + jnp.log(-jnp.expm1(-dt0)),
        "A_log": jnp.log(jax.random.uniform(ks[12], (DEPTH, 2, N_SSD_HEADS), jnp.float32, 1.0, 16.0)),
        "d_skip": 1.0 + nrm(ks[13], (DEPTH, N_SSD_HEADS), 0.02),
        "ssd_norm_g": 1.0 + nrm(ks[14], (DEPTH, D_SSD), 0.02),
        "cm_ln_g": 1.0 + nrm(ks[15], (DEPTH, D_CM), 0.02),
        "cm_ln_b": nrm(ks[16], (DEPTH, D_CM), 0.02),
        "cm_w_s": nrm(ks[17], (DEPTH, N_CM_HEADS, CM_CHUNK, CM_CHUNK), CM_CHUNK ** -0.5),
        "cm_b_s": 1.0 + nrm(ks[18], (DEPTH, N_CM_HEADS, CM_CHUNK), 0.02),
        "w_out": nrm(ks[19], (DEPTH, D_INNER, D_MODEL), D_INNER ** -0.5),
        "norm2_g": 1.0 + nrm(ks[20], (DEPTH, D_MODEL), 0.02),
        "w_ff1": nrm(ks[21], (DEPTH, D_MODEL, D_FF), D_MODEL ** -0.5),
        "w_ff2": nrm(ks[22], (DEPTH, D_FF, D_MODEL), D_FF ** -0.5),
        "final_norm_g": 1.0 + nrm(ks[23], (D_MODEL,), 0.02),
    }


def reference(x_prompt, x_sample, state_ssd, c, c_ctx, w_ada, b_ada, norm1_g, w_in, conv_w, conv_b,
              dt_bias, A_log, d_skip, ssd_norm_g, cm_ln_g, cm_ln_b, cm_w_s, cm_b_s, w_out, norm2_g,
              w_ff1, w_ff2, final_norm_g):
    layer_params = (w_ada, b_ada, norm1_g, w_in, conv_w, conv_b, dt_bias, A_log, d_skip, ssd_norm_g,
                    cm_ln_g, cm_ln_b, cm_w_s, cm_b_s, w_out, norm2_g, w_ff1, w_ff2)
    bp = x_prompt.shape[0]
    xp = x_prompt
    cond_ctx = jnp.broadcast_to(c_ctx, (bp, D_MODEL))
    h0_ctx = jnp.zeros((bp, 2, N_SSD_HEADS, SSD_HEAD_DIM, D_STATE), x_prompt.dtype)
    ctx_states = []
    for l in range(DEPTH):
        xp, st = trunk_layer(xp, cond_ctx, h0_ctx, *[w[l] for w in layer_params])
        ctx_states.append(st)
    y_prompt = rmsnorm(xp, final_norm_g)
    new_state_ssd = jnp.stack(ctx_states, axis=1)
    xs = x_sample + sincos_2d(x_sample.shape[1], x_sample.dtype)[None]
    for l in range(DEPTH):
        xs, _ = trunk_layer(xs, c, state_ssd[:, l], *[w[l] for w in layer_params])
    y_sample = rmsnorm(xs, final_norm_g)
    return (y_prompt, y_sample, new_state_ssd)
```

```python
import contextlib
import math
import numpy as np
import concourse.bass as bass
import concourse.mybir as mybir
from concourse.bass_utils import run_bass_kernel_spmd

F32 = mybir.dt.float32
BF16 = mybir.dt.bfloat16
I32 = mybir.dt.int32
AF = mybir.ActivationFunctionType
ALU = mybir.AluOpType

D = 1024
NIN = 5152
EPS = 1e-6
C_Z, C_X, C_B, C_C, C_DT, C_U, C_V = 0, 1024, 2048, 2560, 3072, 3104, 4128
NEG = -30000.0


class Dep:
    __slots__ = ("name", "writer", "readers")

    def __init__(self, name):
        self.name = name
        self.writer = None
        self.readers = []


class Sched:
    ENGS = ("pe", "act", "dve", "pool", "sp")

    def __init__(self, nc):
        self.nc = nc
        self.ops = {e: [] for e in self.ENGS}
        self.count = {e: 0 for e in self.ENGS}
        self.dma_count = {}
        self.waited = {e: {} for e in self.ENGS}
        self.sem_keys = list(self.ENGS)
        self.needed = {e: set() for e in self.ENGS}

    def _need(self, eng, tok, waits):
        if tok is None:
            return
        key, val, src = tok
        if src == "pe" and eng == "pe":
            return
        cur = self.waited[eng].get(key, 0)
        if cur >= val:
            return
        self.waited[eng][key] = val
        if src != "dma":
            self.needed[key].add(val)
        waits.append((key, val, src))

    def op(self, eng, fn, reads=(), writes=(), dma=None):
        waits = []
        for d in reads:
            self._need(eng, d.writer, waits)
        for d in writes:
            self._need(eng, d.writer, waits)
            for r in d.readers:
                self._need(eng, r, waits)
        if dma is None:
            self.count[eng] += 1
            tok = (eng, self.count[eng], eng)
            inc = (eng, 1)
        else:
            if dma not in self.dma_count:
                self.dma_count[dma] = 0
                self.sem_keys.append(dma)
            self.dma_count[dma] += 16
            tok = (dma, self.dma_count[dma], "dma")
            inc = (dma, 16)
        for d in reads:
            d.readers.append(tok)
        for d in writes:
            d.writer = tok
            d.readers = []
        self.ops[eng].append((waits, fn, inc, tok[1]))
        return tok

    def wait_all(self, eng, toks):
        waits = []
        for t in toks:
            self._need(eng, t, waits)
        self.ops[eng].append((waits, None, None, None))

    def emit(self):
        nc = self.nc
        import bisect
        ranks = {e: sorted(self.needed[e]) for e in self.ENGS}

        def wval(key, val, src):
            if src == "dma":
                return val
            return bisect.bisect_right(ranks[key], val)

        with contextlib.ExitStack() as st:
            sems = {}
            for k in self.sem_keys:
                sems[k] = st.enter_context(nc.semaphore("s_" + k))
            block = st.enter_context(nc.Block())

            def run(engname, handle):
                for waits, fn, inc, seq in self.ops[engname]:
                    for (k, v, src) in waits:
                        handle.wait_ge(sems[k], wval(k, v, src))
                    if fn is not None:
                        ins = fn(handle)
                        if inc[1] == 16 or seq in self.needed[engname]:
                            ins.then_inc(sems[inc[0]], inc[1])

            @block.tensor
            def _(e):
                run("pe", e)

            @block.scalar
            def _(e):
                run("act", e)

            @block.vector
            def _(e):
                run("dve", e)

            @block.gpsimd
            def _(e):
                run("pool", e)

            @block.sync
            def _(e):
                run("sp", e)


class Arena:
    BASE = 16640
    END = 229312

    def __init__(self, nc):
        self.nc = nc
        self.cur = self.BASE
        self.n = 0

    def alloc(self, name, shape, dtype):
        esz = 4 if dtype in (F32, I32) else 2
        size = esz * int(np.prod(shape[1:]))
        off = (self.cur + 31) // 32 * 32
        assert off + size <= self.END, f"SBUF overflow allocating {name}: {off}+{size}"
        self.cur = off + size
        self.n += 1
        return self.nc.alloc_sbuf_tensor_at(f"{name}_{self.n}", list(shape), dtype, offset=off)

    def mark(self):
        return self.cur

    def reset(self, m):
        self.cur = m


class T:
    def __init__(self, h, name):
        self.h = h
        self.d = Dep(name)

    def __getitem__(self, k):
        return self.h[k]


def build_program():
    nc = bass.Bass("TRN2", target_bir_lowering=False)
    S = Sched(nc)
    A = Arena(nc)

    def din(name, shape, dt=F32):
        return nc.dram_tensor(name, list(shape), dt, kind="ExternalInput").ap()

    def dout(name, shape, dt=F32):
        return nc.dram_tensor(name, list(shape), dt, kind="ExternalOutput").ap()

    def dscr(name, shape, dt=F32):
        return nc.dram_tensor(name, list(shape), dt, kind="Internal").ap()

    xs_d = din("xs", [4096, D])
    xp_d = din("xp", [512, D])
    condT_d = din("condT", [128, 16])
    st_d = din("st_in", [2, 1024, 128])
    wada_d = din("w_ada", [D, 6144])
    bada_d = din("b_ada", [1, 6144])
    vecs_d = din("vecs", [6, D])
    win_d = din("w_in", [D, NIN])
    convw_d = din("conv_wT", [128, 80])
    convb_d = din("conv_b", [1, 2048])
    convbT_d = din("conv_bT", [128, 16])
    dtb_d = din("dtb", [1, 32])
    alog_d = din("alog", [1, 32])
    dskip_d = din("dskip", [1, 16])
    wsT_d = din("wsT", [128, 1024])
    bs_d = din("bs", [128, 8])
    wout_d = din("w_out", [2048, D])
    wff1_d = din("w_ff1", [D, 4096])
    wff2_d = din("w_ff2", [4096, D])
    rowpos_d = din("rowpos", [64, 1])
    colpos_d = din("colpos", [128, 1])

    yp_d = dout("yp", [512, D])
    ys_d = dout("ys", [2048, D])
    nst_d = dout("nst", [2, 2, 1024, 128])

    modscr = dscr("modscr", [2, 128, 6144])
    hbscr = dscr("hbscr", [20, 128, 1024], BF16)
    catscr = dscr("catscr", [2560, 2048], BF16)
    x1scr = dscr("x1scr", [2560, D])
    rtab = dscr("rtab", [64, 512])
    xsscr = dscr("xsscr", [20, 128, 1024], BF16)
    bscr = dscr("bscr", [20, 128, 512], BF16)

    out_toks = []

    def deps(xs):
        return [x.d if isinstance(x, T) else x for x in xs]

    REC = [None]

    def emit(eng, fn, reads, writes, dma=None, tag=None):
        if REC[0] is not None:
            REC[0].append((eng, fn, list(reads), list(writes), dma, tag))
            return None
        return S.op(eng, fn, reads, writes, dma=dma)

    def record(f):
        assert REC[0] is None
        REC[0] = []
        try:
            f()
            return REC[0]
        finally:
            REC[0] = None

    cur_set = [None]
    SLACK = 0.12

    def merge(lists, bias=None):
        if bias is None:
            bias = [0.0] * len(lists)
        keep = [j for j, l in enumerate(lists) if l]
        bias = [bias[j] for j in keep]
        lists = [lists[j] for j in keep]
        idx = [0] * len(lists)
        while True:
            cands = []
            for j, l in enumerate(lists):
                if idx[j] < len(l):
                    cands.append((idx[j] / len(l) + bias[j], j))
            if not cands:
                break
            cands.sort()
            pick = cands[0][1]
            tag0 = lists[pick][idx[pick]][5]
            if tag0 is not None and cur_set[0] is not None and tag0 != cur_set[0]:
                for fr, j in cands[1:]:
                    tg = lists[j][idx[j]][5]
                    if fr <= cands[0][0] + SLACK and (tg is None or tg == cur_set[0]):
                        pick = j
                        break
            eng, fn, reads, writes, dma, tag = lists[pick][idx[pick]]
            idx[pick] += 1
            if tag is not None:
                cur_set[0] = tag
            S.op(eng, fn, reads, writes, dma=dma)

    def MM(out, lhsT, rhs, start, stop, reads, writes):
        emit("pe", lambda e: e.matmul(out, lhsT=lhsT, rhs=rhs, start=start, stop=stop), deps(reads), deps(writes))

    def TR(out, in_, ident_ap, reads, writes):
        emit("pe", lambda e: e.transpose(out=out, in_=in_, identity=ident_ap), deps(reads), deps(writes))

    ACT_TAGS = {AF.Silu: "silu", AF.Gelu_apprx_tanh: "gelu", AF.Exp: "expln", AF.Ln: "expln", AF.Sin: "sin", AF.Sqrt: "sqrt"}

    def ACT(out, in_, func, reads, writes, **kw):
        emit("act", lambda e: e.activation(out=out, in_=in_, func=func, **kw), deps(reads), deps(writes), tag=ACT_TAGS.get(func))

    def TT(eng, out, in0, in1, op, reads, writes):
        emit(eng, lambda e: e.tensor_tensor(out=out, in0=in0, in1=in1, op=op), deps(reads), deps(writes))

    def TS(eng, out, in0, s1, s2, op0, op1, reads, writes):
        if s2 is None:
            emit(eng, lambda e: e.tensor_scalar(out=out, in0=in0, scalar1=s1, scalar2=None, op0=op0), deps(reads), deps(writes))
        else:
            emit(eng, lambda e: e.tensor_scalar(out=out, in0=in0, scalar1=s1, scalar2=s2, op0=op0, op1=op1), deps(reads), deps(writes))

    def STT(out, in0, scalar, in1, op0, op1, reads, writes):
        emit("dve", lambda e: e.scalar_tensor_tensor(out=out, in0=in0, scalar=scalar, in1=in1, op0=op0, op1=op1), deps(reads), deps(writes))

    def CP(eng, out, in_, reads, writes):
        if eng == "act":
            emit("act", lambda e: e.copy(out=out, in_=in_), deps(reads), deps(writes))
        else:
            emit(eng, lambda e: e.tensor_copy(out=out, in_=in_), deps(reads), deps(writes))

    def MSET(eng, ap, val, writes):
        emit(eng, lambda e: e.memset(ap, val), [], deps(writes))

    def DMA(eng, out, in_, key, reads, writes):
        return emit(eng, lambda e: e.dma_start(out=out, in_=in_), deps(reads), deps(writes), dma=key)

    def RECIP(out, in_, reads, writes):
        emit("dve", lambda e: e.reciprocal(out=out, in_=in_), deps(reads), deps(writes))

    def ASEL(out, in_, pattern, op, fill, base, cm, reads, writes):
        emit("pool", lambda e: e.affine_select(out=out, in_=in_, pattern=pattern, compare_op=op, fill=fill,
                                               base=base, channel_multiplier=cm), deps(reads), deps(writes))

    def sbt(name, shape, dt=F32):
        return T(A.alloc(name, shape, dt), name)

    pW0h = nc.alloc_psum_tensor("pW0", [128, 1024], F32)
    pW1h = nc.alloc_psum_tensor("pW1", [128, 1024], F32)
    pYh = nc.alloc_psum_tensor("pY", [128, 1024], F32)
    pTh = nc.alloc_psum_tensor("pT", [128, 1024], BF16)
    pSh = nc.alloc_psum_tensor("pS", [128, 512], F32)
    pW0, pW1, pY, pT, pS = T(pW0h, "pW0"), T(pW1h, "pW1"), T(pYh, "pY"), T(pTh, "pT"), T(pSh, "pS")
    halves = [(pW0, 0, Dep("pW0a")), (pW0, 512, Dep("pW0b")), (pW1, 0, Dep("pW1a")), (pW1, 512, Dep("pW1b"))]
    hctr = [0, 0, 0]

    def next_half(pair=2):
        if pair == 2:
            h = halves[hctr[2] % 4]
        else:
            h = halves[pair * 2 + hctr[pair] % 2]
        hctr[pair] += 1
        return h

    identf = sbt("identf", [128, 128])
    ident = sbt("ident", [128, 128], BF16)
    onesf = sbt("onesf", [128, 128])
    eps_t = sbt("eps_t", [128, 1])
    PC = sbt("PC", [128, 512])
    m_core = A.mark()
    tri_f = sbt("tri_f", [128, 128])
    tri_r = sbt("tri_r", [128, 128])
    negm = sbt("negm", [128, 2, 128], BF16)
    SEL2 = sbt("SEL2", [128, 16, 128], BF16)
    A_b = sbt("A_b", [128, 2, 32])
    dtb_b = sbt("dtb_b", [128, 2, 32])
    dskip_b = sbt("dskip_b", [128, 16])
    convb128 = sbt("convb128", [128, 128], BF16)
    convbT = sbt("convbT", [128, 16])
    convdiag = sbt("convdiag", [128, 80, 128], BF16)
    m_c1 = A.mark()
    zerosf = sbt("zerosf", [128, 128])
    negf = sbt("negf", [128, 2, 128])
    self32 = sbt("self32", [128, 16])
    convwT = sbt("convwT", [128, 80])

    MSET("pool", onesf[:], 1.0, [onesf])
    MSET("pool", zerosf[:], 0.0, [zerosf])
    MSET("pool", eps_t[:], EPS, [eps_t])
    ASEL(identf[:], onesf[:], [[-1, 128]], ALU.is_equal, 0.0, 0, 1, [onesf], [identf])
    CP("dve", ident[:], identf[:], [identf], [ident])
    ASEL(tri_f[:], onesf[:], [[1, 128]], ALU.is_ge, 0.0, 0, -1, [onesf], [tri_f])
    ASEL(tri_r[:], onesf[:], [[-1, 128]], ALU.is_ge, 0.0, 0, 1, [onesf], [tri_r])
    ASEL(negf[:, 0, :], zerosf[:], [[1, 128]], ALU.is_ge, NEG, 0, -1, [zerosf], [negf])
    ASEL(negf[:, 1, :], zerosf[:], [[-1, 128]], ALU.is_ge, NEG, 0, 1, [zerosf], [negf])
    CP("dve", negm[:], negf[:], [negf], [negm])
    MSET("pool", self32[:], 0.0, [self32])
    for blk in range(2):
        ASEL(self32[blk * 32:(blk + 1) * 32, :], onesf[blk * 32:(blk + 1) * 32, 0:16], [[-1, 16]], ALU.is_equal, 0.0, 0, 1, [onesf, self32], [self32])
    CP("dve", SEL2[:], self32[:].unsqueeze(2).to_broadcast([128, 16, 128]), [self32], [SEL2])
    MSET("pool", convb128[:], 0.0, [convb128])
    DMA("sp", convbT[:], convbT_d, "c_convbT", [], [convbT])
    DMA("pool", convb128[0:16, :], convb_d.rearrange("o (a b) -> (o a) b", a=16), "c_convb", [], [convb128])
    MSET("pool", A_b[:], 0.0, [A_b])
    MSET("pool", dtb_b[:], 0.0, [dtb_b])
    for d_ in range(2):
        DMA("sp", A_b[:, d_, 0:16], alog_d[:, d_ * 16:(d_ + 1) * 16].partition_broadcast(128), "c_A_b", [], [A_b])
        DMA("sp", dtb_b[:, d_, 0:16], dtb_d[:, d_ * 16:(d_ + 1) * 16].partition_broadcast(128), "c_dtb_b", [], [dtb_b])
    DMA("sp", dskip_b[:], dskip_d.partition_broadcast(128), "c_dskip", [], [dskip_b])
    ACT(A_b[:, :, 0:16], A_b[:, :, 0:16], AF.Exp, [A_b], [A_b])
    TS("dve", A_b[:, :, 0:16], A_b[:, :, 0:16], -1.0, None, ALU.mult, None, [A_b], [A_b])
    DMA("sp", convwT[:], convw_d, "c_convwT", [], [convwT])
    TT("dve", convdiag[:], identf[:].unsqueeze(1).to_broadcast([128, 80, 128]),
       convwT[:].unsqueeze(2).to_broadcast([128, 80, 128]), ALU.mult, [identf, convwT], [convdiag])

    m_persist = A.mark()
    d_setup_tmp = [zerosf, negf, self32, convwT]

    omega = sbt("omega", [128, 256])
    io_i = sbt("io_i", [128, 256], I32)
    S.op("pool", lambda e: e.iota(io_i[:], pattern=[[1, 256]], base=0, channel_multiplier=0), [], [io_i.d])
    CP("dve", omega[:], io_i[:], [io_i], [omega])
    ACT(omega[:], omega[:], AF.Exp, [omega], [omega], scale=-math.log(10000.0) / 256.0)
    pos_sb = sbt("pos_sb", [128, 1])
    ang = sbt("ang", [128, 512])
    ki = sbt("ki", [128, 512], I32)
    kf = sbt("kf", [128, 512])
    Rt = sbt("Rt", [128, 512])

    def sincos(dst_t, pos_dram, nrows):
        DMA("sp", pos_sb[0:nrows, :], pos_dram, "c_pos", [], [pos_sb])
        r = slice(0, nrows)
        TS("dve", ang[r, 0:256], omega[r, :], pos_sb[r, 0:1], 1.0 / (2 * math.pi), ALU.mult, ALU.mult, [omega, pos_sb], [ang])
        TS("dve", ang[r, 256:512], ang[r, 0:256], 0.25, None, ALU.add, None, [ang], [ang])
        CP("dve", ki[r, :], ang[r, :], [ang], [ki])
        CP("dve", kf[r, :], ki[r, :], [ki], [kf])
        TT("dve", ang[r, :], ang[r, :], kf[r, :], ALU.subtract, [ang, kf], [ang])
        TS("dve", ang[r, :], ang[r, :], -0.5, 0.5, ALU.max, ALU.min, [ang], [ang])
        ACT(dst_t[r, :], ang[r, :], AF.Sin, [ang], [dst_t], scale=6.28318)

    sincos(PC, colpos_d, 128)
    sincos(Rt, rowpos_d, 64)
    DMA("sp", rtab, Rt[0:64, :], "c_rtab", [Rt], [])
    d_rtab = Dep("rtab")
    d_rtab.writer = ("c_rtab", S.dma_count["c_rtab"], "dma")
    setup_tmp_toks = []
    for t_ in d_setup_tmp + [omega, io_i, pos_sb, ang, ki, kf, Rt]:
        if t_.d.writer is not None:
            setup_tmp_toks.append(t_.d.writer)
        setup_tmp_toks.extend(t_.d.readers)
    A.reset(m_c1)

    Win = sbt("Win", [128, 8, NIN], BF16)
    Win.d.readers.extend(setup_tmp_toks)
    win_v = win_d.rearrange("(kt p) n -> p kt n", p=128)
    def issue_win():
        for kt in range(8):
            DMA("pool", Win[:, kt, :], win_v[:, kt, :], "w_in", [], [Win])
    m_win = A.mark()

    condT = sbt("condT", [128, 16])
    condL = sbt("condL", [128, 16, 128], BF16)
    DMA("sp", condT[:], condT_d, "c_condT", [], [condT])
    ACT(condT[:], condT[:], AF.Silu, [condT], [condT])
    CP("dve", condL[:], condT[:].unsqueeze(2).to_broadcast([128, 16, 128]), [condT], [condL])
    NWB = 4
    wab = [sbt(f"wab{i}", [128, 8, 512], BF16) for i in range(NWB)]
    bab = [sbt(f"bab{i}", [128, 512]) for i in range(NWB)]
    mev = [sbt(f"mev{i}", [128, 512]) for i in range(4)]
    wada_v = wada_d.rearrange("(kt p) n -> p kt n", p=128)
    for blk in range(12):
        b2 = blk % NWB
        cs = slice(blk * 512, (blk + 1) * 512)
        DMA("pool", wab[b2][:], wada_v[:, :, cs], f"wada{b2}", [], [wab[b2]])
        DMA("sp", bab[b2][:], bada_d[:, cs].partition_broadcast(128), f"bada{b2}", [], [bab[b2]])
        if blk == NWB - 1:
            issue_win()
        for ci in range(2):
            pt_, off, dd = next_half()
            for kt in range(8):
                MM(pt_[:, off:off + 512], condL[:, ci * 8 + kt, :], wab[b2][:, kt, :], kt == 0, kt == 7, [condL, wab[b2]], [dd])
            ev = mev[(blk * 2 + ci) % 4]
            TT("dve", ev[:], pt_[:, off:off + 512], bab[b2][:], ALU.add, [dd, bab[b2]], [ev])
            DMA("sp", modscr[ci, :, cs], ev[:], f"modst{(blk * 2 + ci) % 4}", [ev], [])
    d_modscr = Dep("modscr")
    mod_toks = [(f"modst{i}", S.dma_count[f"modst{i}"], "dma") for i in range(4)]
    A.reset(m_win)
    bar0 = [(e_, S.count[e_], e_) for e_ in S.ENGS if S.count[e_] > 0] + [(k_, v_, 'dma') for k_, v_ in S.dma_count.items() if k_ != 'w_in']
    _sbt_plain = sbt

    def sbt(name, shape, dt=F32):
        t_ = _sbt_plain(name, shape, dt)
        t_.d.readers.extend(bar0)
        return t_

    G1 = sbt("G1", [128, D], BF16)
    sh1 = sbt("sh1", [128, D], BF16)
    t1 = sbt("t1", [128, D])
    ysb = sbt("ysb", [128, D])
    S.wait_all("sp", mod_toks)
    S.wait_all("pool", mod_toks)

    def load_mod1(ci):
        DMA("sp", t1[:], vecs_d[0:1, :].partition_broadcast(128), "c_t1", [], [t1])
        DMA("pool", sh1[:], modscr[ci, :, 0:1024], "c_sh1", [], [sh1])
        DMA("sp", ysb[:], modscr[ci, :, 1024:2048], "c_ysb", [], [ysb])
        STT(G1[:], ysb[:], 1.0, t1[:], ALU.add, ALU.mult, [ysb, t1], [G1])

    xt = [sbt(f"xt{i}", [128, D]) for i in range(2)]
    posL = [sbt(f"posL{i}", [128, 512]) for i in range(2)]
    hb = sbt("h", [128, D], BF16)
    hTx = [sbt(f"hTx{i}", [128, 8, 132], BF16) for i in range(3)]
    rawT = sbt("rawT", [128, 16, 132], BF16)
    xs_tm = [sbt(f"xs_tm{i}", [128, D], BF16) for i in range(2)]
    B_tm = [sbt(f"B_tm{i}", [128, 512], BF16) for i in range(2)]
    C_tm = sbt("C_tm", [128, 512], BF16)
    dtr2 = [sbt(f"dtr{i}", [128, 2, 32]) for i in range(2)]
    st_ = {n: sbt(n, [128, 2, 32]) for n in ("dt", "dtA", "nac", "ea", "cd", "dte", "wgt", "tmpd")}
    for t_ in list(st_.values()) + dtr2:
        MSET("pool", t_[:], 0.0, [t_])
    ss = sbt("ss", [128, 4])
    ssB = sbt("ssB", [128, 4])
    xdt = [sbt(f"xdt{i}", [128, D], BF16) for i in range(2)]
    xw = xdt[1]
    H = sbt("H", [128, D])
    Hbf = [sbt(f"Hbf{i}", [128, D], BF16) for i in range(2)]
    Hst1 = Hbf[1]
    stl = t1
    pos_ctr = [0]

    def load_x(xtile, src_rows, pos_chunk):
        DMA("sp", xtile[:], src_rows, f"ld_{xtile.d.name}", [], [xtile])
        if pos_chunk is not None:
            pl = posL[pos_ctr[0] % len(posL)]
            pos_ctr[0] += 1
            for e in range(2):
                DMA("sp", pl[e * 64:(e + 1) * 64, :], rtab[2 * pos_chunk + e:2 * pos_chunk + e + 1, :].partition_broadcast(64),
                    f"ld_{pl.d.name}", [d_rtab], [pl])
            return pl
        return None

    def add_pos(xtile, pl):
        if pl is not None:
            TT("pool", xtile[:, 0:512], xtile[:, 0:512], pl[:], ALU.add, [xtile, pl], [xtile])
            TT("pool", xtile[:, 512:1024], xtile[:, 512:1024], PC[:], ALU.add, [xtile, PC], [xtile])

    def rms_rstd(src_ap, src_t, sst, col, dump_ap, dump_t):
        ACT(dump_ap, src_ap, AF.Square, [src_t], [dump_t, sst], accum_out=sst[:, col:col + 1])
        ACT(sst[:, col:col + 1], sst[:, col:col + 1], AF.Ln, [sst], [sst], scale=1.0 / D, bias=eps_t[:, 0:1])
        ACT(sst[:, col:col + 1], sst[:, col:col + 1], AF.Exp, [sst], [sst], scale=-0.5)

    def FE(xtile, pl, slot):
        add_pos(xtile, pl)
        rms_rstd(xtile[:], xtile, ss, 0, hb[:], hb)
        STT(xtile[:], xtile[:], ss[:, 0:1], G1[:], ALU.mult, ALU.mult, [xtile, ss, G1], [xtile])
        TT("dve", hb[:], xtile[:], sh1[:], ALU.add, [xtile, sh1], [hb])
        for kt in range(8):
            TR(pT[:, kt * 128:(kt + 1) * 128], hb[:, kt * 128:(kt + 1) * 128], ident[:], [hb, ident], [pT])
        CP("act", hTx[slot][:, :, 2:130], pT[:].rearrange("p (a b) -> p a b", a=8), [pT], [hTx[slot]])

    def halo_copy(dst_slot, dst_lo, src_slot, src_lo):
        CP("pool", hTx[dst_slot][:, :, dst_lo:dst_lo + 2], hTx[src_slot][:, :, src_lo:src_lo + 2], [hTx[src_slot]], [hTx[dst_slot]])

    def halo_zero(slot, lo):
        MSET("pool", hTx[slot][:, :, lo:lo + 2], 0.0, [hTx[slot]])

    def raw_only(slot, ftiles):
        hx = hTx[slot]
        for g0 in range(0, len(ftiles), 3):
            grp = ftiles[g0:g0 + 3]
            pt_, off, dd = next_half(0)
            for j, ft in enumerate(grp):
                for kt in range(8):
                    MM(pt_[:, off + j * 132: off + (j + 1) * 132], Win[:, kt, C_X + ft * 128: C_X + (ft + 1) * 128], hx[:, kt, :],
                       kt == 0, kt == 7, [Win, hx], [dd])
            n = len(grp)
            CP("act", rawT[:, grp[0]:grp[0] + n, :], pt_[:, off: off + n * 132].rearrange("p (a b) -> p a b", a=n), [dd], [rawT])

    def raw_and_conv(slot, ftiles, par):
        hx = hTx[slot]
        for g0 in range(0, len(ftiles), 3):
            grp = ftiles[g0:g0 + 3]
            pt_, off, dd = next_half(0)
            for j, ft in enumerate(grp):
                for kt in range(8):
                    MM(pt_[:, off + j * 132: off + (j + 1) * 132], Win[:, kt, C_X + ft * 128: C_X + (ft + 1) * 128], hx[:, kt, :],
                       kt == 0, kt == 7, [Win, hx], [dd])
            n = len(grp)
            CP("act", rawT[:, grp[0]:grp[0] + n, :], pt_[:, off: off + n * 132].rearrange("p (a b) -> p a b", a=n), [dd], [rawT])
        for g0 in range(0, len(ftiles), 4):
            grp = ftiles[g0:g0 + 4]
            pt_, off, dd = next_half(0)
            for j, ft in enumerate(grp):
                o = pt_[:, off + j * 128: off + (j + 1) * 128]
                for k in range(5):
                    MM(o, rawT[:, ft, k:k + 128], convdiag[:, ft * 5 + k, :], k == 0, False, [rawT, convdiag], [dd])
                MM(o, SEL2[:, ft, :], convb128[:], False, True, [SEL2, convb128], [dd])
            ft0 = grp[0]
            if ft0 < 8:
                dst, dt_ = xs_tm[par][:, ft0 * 128:(ft0 + 4) * 128], xs_tm[par]
            elif ft0 < 12:
                dst, dt_ = B_tm[par][:], B_tm[par]
            else:
                dst, dt_ = C_tm[:], C_tm
            ACT(dst, pt_[:, off:off + 512], AF.Silu, [dd], [dt_])

    def dt_raw(slot, dirs, par):
        hx = hTx[slot]
        pt_, off, dd = next_half(0)
        ds = slice(dirs[0], dirs[-1] + 1)
        nd = len(dirs)
        c0 = C_DT + dirs[0] * 16
        for kt in range(8):
            MM(pt_[:, off: off + 16 * nd], hx[:, kt, 2:130], Win[:, kt, c0: c0 + 16 * nd], kt == 0, kt == 7, [hx, Win], [dd])
        TT("dve", dtr2[par][:, ds, 0:16], pt_[:, off:off + 16 * nd].rearrange("p (a b) -> p a b", a=nd), dtb_b[:, ds, 0:16], ALU.add, [dd, dtb_b], [dtr2[par]])

    def dt_stuff(dirs, par):
        ds = slice(dirs[0], dirs[-1] + 1)
        v = lambda t: t[:, ds, 0:16]
        pv = lambda c0: pS2[:, c0:c0 + 64].rearrange("p (a b) -> p a b", a=2)[:, ds, 0:16]
        dtr = dtr2[par]
        dt, dtA, nac, ea, cd, dte, wgt, tmpd = [st_[n] for n in ("dt", "dtA", "nac", "ea", "cd", "dte", "wgt", "tmpd")]
        ACT(v(dtr), v(dtr), AF.Exp, [dtr], [dtr])
        ACT(v(dt), v(dtr), AF.Ln, [dtr], [dt], bias=1.0)
        TT("dve", v(dtA), v(dt), v(A_b), ALU.mult, [dt, A_b], [dtA])
        for d_ in dirs:
            tri = tri_f if d_ == 0 else tri_r
            MM(pS2[:, 0 + d_ * 32: 0 + d_ * 32 + 16], tri[:], dtA[:, d_, 0:16], True, True, [tri, dtA], [dS2])
            MM(pS2[:, 64 + d_ * 32: 64 + d_ * 32 + 16], onesf[:], dtA[:, d_, 0:16], True, True, [onesf, dtA], [dS2])
        ACT(v(ea), pv(0), AF.Exp, [dS2], [ea])
        ACT(v(cd), pv(64), AF.Exp, [dS2], [cd])
        ACT(v(nac), pv(0), AF.Copy, [dS2], [nac], scale=-1.0)
        TT("dve", v(tmpd), pv(64), v(nac), ALU.add, [dS2, nac], [tmpd])
        ACT(v(dte), v(tmpd), AF.Exp, [tmpd], [dte])
        TT("dve", v(wgt), v(dt), v(dte), ALU.mult, [dt, dte], [wgt])

    pS2 = pS.h
    dS2 = pS.d

    W0D = [halves[0][2], halves[1][2]]
    W1D = [halves[2][2], halves[3][2]]

    def state_update(d_, Sps, Sdeps):
        cd = st_["cd"]
        TT("dve", H[:].rearrange("p (h q) -> p h q", h=16), H[:].rearrange("p (h q) -> p h q", h=16),
           cd[:, d_, 0:16].unsqueeze(2).to_broadcast([128, 16, 64]), ALU.mult, [H, cd], [H])
        TT("dve", H[:], H[:], Sps[:], ALU.add, [H] + Sdeps, [H])

    def local_state(d_, Sps, Sdeps, par):
        wgt = st_["wgt"]
        TT("dve", xw[:].rearrange("p (h q) -> p h q", h=16), xs_tm[par][:].rearrange("p (h q) -> p h q", h=16),
           wgt[:, d_, 0:16].unsqueeze(2).to_broadcast([128, 16, 64]), ALU.mult, [xs_tm[par], wgt], [xw])
        for g in range(4):
            MM(Sps[:, g * 256:(g + 1) * 256], B_tm[par][:, g * 128:(g + 1) * 128], xw[:, g * 256:(g + 1) * 256], True, True, [B_tm[par], xw], Sdeps)

    def h_init(seg, d_):
        if seg["kind"] == "prompt":
            MSET("pool", H[:], 0.0, [H])
        else:
            for half in range(2):
                DMA("sp", stl[:, 0:512].rearrange("p (a b) -> p a b", a=4), st_d[d_, half * 512:(half + 1) * 512, :].rearrange("(a p) n -> p a n", p=128),
                    "ldst", [], [stl])
                for a in range(4):
                    emit("pe", lambda e, a=a: e.transpose(out=pY[:, a * 128:(a + 1) * 128], in_=stl[:, a * 128:(a + 1) * 128], identity=identf[:]),
                         deps([stl, identf]), deps([pY]))
                CP("act", H[:, half * 512:(half + 1) * 512], pY[:, 0:512], [pY], [H])

    def h_out(seq, d_):
        for half in range(2):
            for a in range(4):
                c0 = half * 512 + a * 128
                emit("pe", lambda e, a=a, c0=c0: e.transpose(out=pY[:, a * 128:(a + 1) * 128], in_=H[:, c0:c0 + 128], identity=identf[:]),
                     deps([H, identf]), deps([pY]))
            CP("act", stl[:, 0:512], pY[:, 0:512], [pY], [stl])
            DMA("sp", nst_d[seq, d_, half * 512:(half + 1) * 512, :].rearrange("(a p) n -> p a n", p=128),
                stl[:, 0:512].rearrange("p (a b) -> p a b", a=4), "st_nst", [stl], [])

    segs = [
        dict(kind="prompt", seq=0, ci=0, nch=2, own=2, gc0=0, xsrc=lambda c: xp_d[c * 128:(c + 1) * 128, :], pos=False),
        dict(kind="prompt", seq=1, ci=0, nch=2, own=2, gc0=2, xsrc=lambda c: xp_d[256 + c * 128:256 + (c + 1) * 128, :], pos=False),
        dict(kind="sample", seq=None, ci=1, nch=32, own=16, gc0=4, xsrc=lambda c: xs_d[c * 128:(c + 1) * 128, :], pos=True),
    ]

    slot_ctr = [0]
    xt_ctr = [0]
    B_BIAS = 0.15

    def run_pass(seg, order, be_list, PROJ, BEB, CM=None, FIN=None):
        fwd = order[0] < order[-1] if len(order) > 1 else True
        n = len(order)
        loads = {}
        slots = {}

        def issue_load(i):
            if 0 <= i < n:
                xtile = xt[xt_ctr[0] % 2]
                xt_ctr[0] += 1
                pl = load_x(xtile, seg["xsrc"](order[i]), order[i] if seg["pos"] else None)
                loads[i] = (xtile, pl)

        def fe_thread(i):
            xtile, pl = loads.pop(i)
            sl = slot_ctr[0] % 3
            slot_ctr[0] += 1
            slots[i] = sl
            FE(xtile, pl, sl)
            lo_in, lo_out = (0, 130) if fwd else (130, 0)
            src_prev, src_cur = (128, 2) if fwd else (2, 128)
            seq_first = (order[i] == 0) if fwd else (order[i] == seg["nch"] - 1)
            if i == 0:
                assert seq_first, "pass must start at a sequence end"
                halo_zero(sl, lo_in)
            else:
                halo_copy(sl, lo_in, slots[i - 1], src_prev)
                halo_copy(slots[i - 1], lo_out, sl, src_cur)
            seq_last = (order[i] == seg["nch"] - 1) if fwd else (order[i] == 0)
            if seq_last:
                halo_zero(sl, lo_out)

        issue_load(0)
        for s in range(-3, n):
            issue_load(s + 4)
            threads = []
            if 0 <= s + 3 < n:
                threads.append(record(lambda: fe_thread(s + 3)))
            do_proj = 0 <= s + 1 < n and order[s + 1] in be_list
            do_b = 0 <= s < n and order[s] in be_list

            def t2():
                if do_proj:
                    PROJ(seg, order[s + 1], slots[s + 1], (s + 1) % 2)
                if do_b and CM is not None:
                    CM(seg, order[s], s % 2)
            biases = [0.0] * len(threads)
            threads.append(record(t2))
            biases.append(0.0)
            if do_b:
                threads.append(record(lambda: BEB(seg, order[s], s % 2)))
                biases.append(B_BIAS)
            merge(threads, biases)
            if do_b and FIN is not None:
                FIN(seg, order[s])

    hbsave_ctr = [0]

    def PROJ_light(seg, c, slot, par):
        raw_and_conv(slot, list(range(12)), par)
        dt_raw(slot, [1], par)
        if c < seg["own"]:
            DMA("sp", xsscr[seg["gc0"] + c], xs_tm[par][:], f"st_{xs_tm[par].d.name}", [xs_tm[par]], [])
            DMA("sp", bscr[seg["gc0"] + c], B_tm[par][:], f"st_{B_tm[par].d.name}", [B_tm[par]], [])

    def B_light(seg, c, par):
        dt_stuff([1], par)
        if c < seg["own"]:
            hbt = Hbf[hbsave_ctr[0] % 2]
            hbsave_ctr[0] += 1
            CP("pool", hbt[:], H[:], [H], [hbt])
            DMA("sp", hbscr[seg["gc0"] + c], hbt[:], f"st_{hbt.d.name}", [hbt], [])
        local_state(1, pY, [pY.d], par)
        state_update(1, pY, [pY.d])

    cur_ci = [None]

    def ensure_mod1(ci):
        if cur_ci[0] != ci:
            load_mod1(ci)
            cur_ci[0] = ci

    for seg in segs:
        ensure_mod1(seg["ci"])
        h_init(seg, 1)
        order = list(range(seg["nch"] - 1, -1, -1))
        run_pass(seg, order, order, PROJ_light, B_light)
        if seg["kind"] == "prompt":
            h_out(seg["seq"], 1)
    hb_toks = [(f"st_{n_}{i}", S.dma_count[f"st_{n_}{i}"], "dma") for i in range(2) for n_ in ("Hbf", "xs_tm", "B_tm")]

    ssdg_b = sbt("ssdg_b", [128, D], BF16)
    lng_b = sbt("lng_b", [128, D], BF16)
    lnb_b = sbt("lnb_b", [128, D], BF16)
    bs_t = sbt("bs_t", [128, 8])
    wsT = sbt("wsT", [128, 1024], BF16)
    DMA("pool", ssdg_b[:], vecs_d[3:4, :].partition_broadcast(128), "c_ssdg_b", [], [ssdg_b])
    DMA("pool", lng_b[:], vecs_d[4:5, :].partition_broadcast(128), "c_lng_b", [], [lng_b])
    DMA("pool", lnb_b[:], vecs_d[5:6, :].partition_broadcast(128), "c_lnb_b", [], [lnb_b])
    DMA("sp", bs_t[:], bs_d, "c_bs_t", [], [bs_t])
    DMA("pool", wsT[:], wsT_d, "c_wsT", [], [wsT])
    sz = [sbt(f"sz{i}", [128, D], BF16) for i in range(2)]
    gu = [sbt(f"gu{i}", [128, D], BF16) for i in range(2)]
    gv = [sbt(f"gv{i}", [128, D], BF16) for i in range(2)]
    BCT = [sbt(f"BCT{i}", [128, 8, 128], BF16) for i in range(2)]
    cbT = sbt("cbT", [128, 4, 128], BF16)
    ACs = sbt("ACs", [128, 2, 128], BF16)
    NACs = sbt("NACs", [128, 2, 128], BF16)
    tmpb = sbt("tmpb", [128, 2, 128], BF16)
    dtA2 = sbt("dtA2", [128, 2, 128])
    for t_ in (ACs, NACs, dtA2):
        MSET("pool", t_[:], 0.0, [t_])
    decT = [sbt(f"decT{i}", [128, 4, 128], BF16) for i in range(2)]
    MT = [sbt(f"MT{i}", [128, 4, 128], BF16) for i in range(2)]
    cat = [sbt("cat0", [128, 2048], BF16)]
    S.wait_all("sp", hb_toks)

    def PROJ_full(seg, c, slot, par):
        hx = hTx[slot]
        gc_ = seg["gc0"] + c
        DMA("sp", xs_tm[par][:], xsscr[gc_], f"ld_{xs_tm[par].d.name}", [], [xs_tm[par]])
        DMA("sp", B_tm[par][:], bscr[gc_], f"ld_{B_tm[par].d.name}", [], [B_tm[par]])
        raw_only(slot, [12, 13, 14, 15])
        pt_, off, dd = next_half(0)
        for g in range(4):
            ft = 12 + g
            for k in range(5):
                MM(pt_[:, off + g * 128: off + (g + 1) * 128], convdiag[:, ft * 5 + k, :], rawT[:, ft, k:k + 128], k == 0, k == 4, [rawT, convdiag], [dd])
        for g in range(4):
            ACT(BCT[par][:, 4 + g, :], pt_[:, off + g * 128: off + (g + 1) * 128], AF.Silu, [dd, convbT], [BCT[par]], bias=convbT[:, 12 + g:13 + g])
        pt_, off, dd = next_half(0)
        pb = pt_[:, off:off + 256].bitcast(BF16)
        for g in range(4):
            TR(pb[:, g * 128:(g + 1) * 128], B_tm[par][:, g * 128:(g + 1) * 128], ident[:], [B_tm[par], ident], [dd])
        CP("act", BCT[par][:, 0:4, :], pb.rearrange("p (a b) -> p a b", a=4), [dd], [BCT[par]])
        for (c0, dst, fn) in ((C_Z, sz[par], AF.Silu), (C_U, gu[par], AF.Gelu_apprx_tanh), (C_V, gv[par], AF.Gelu_apprx_tanh)):
            for nb in range(2):
                pt_, off, dd = next_half(0)
                for kt in range(8):
                    MM(pt_[:, off:off + 512], hx[:, kt, 2:130], Win[:, kt, c0 + nb * 512: c0 + (nb + 1) * 512], kt == 0, kt == 7, [hx, Win], [dd])
                ACT(dst[:, nb * 512:(nb + 1) * 512], pt_[:, off:off + 512], fn, [dd], [dst])
        dt_raw(slot, [0, 1], par)

    cat_a, cat_b = Dep("cat_a"), Dep("cat_b")
    vnb = sbt("vnb", [128, D], BF16)
    cmt = sbt("cmt", [128, D], BF16)
    ssC = sbt("ssC", [128, 4])
    bnst = sbt("bnst2", [128, 2, 6])
    mv = sbt("mv2", [128, 2])

    def CM_full(seg, c, par):
        gu_, gv_ = gu[par], gv[par]
        ct = cat[0]
        for q in range(2):
            emit("dve", lambda e, q=q: e.bn_stats(out=bnst[:, q, :], in_=gv_[:, q * 512:(q + 1) * 512]), deps([gv_]), deps([bnst]))
        emit("dve", lambda e: e.bn_aggr(out=mv[:], in_=bnst[:]), deps([bnst]), deps([mv]))
        ACT(ssC[:, 0:1], mv[:, 1:2], AF.Ln, [mv], [ssC], bias=eps_t[:, 0:1])
        ACT(ssC[:, 0:1], ssC[:, 0:1], AF.Exp, [ssC], [ssC], scale=-0.5)
        TS("dve", cmt[:], gv_[:], mv[:, 0:1], ssC[:, 0:1], ALU.subtract, ALU.mult, [gv_, mv, ssC], [cmt])
        TT("dve", cmt[:], cmt[:], lng_b[:], ALU.mult, [cmt, lng_b], [cmt])
        TT("dve", vnb[:], cmt[:], lnb_b[:], ALU.add, [cmt, lnb_b], [vnb])
        for half in range(2):
            pt_, off, dd = next_half(0)
            for hh in range(4):
                h_ = half * 4 + hh
                MM(pt_[:, off + hh * 128: off + (hh + 1) * 128], wsT[:, h_ * 128:(h_ + 1) * 128], vnb[:, h_ * 128:(h_ + 1) * 128], True, True, [wsT, vnb], [dd])
            TT("dve", cmt[:, half * 512:(half + 1) * 512].rearrange("p (h q) -> p h q", h=4), pt_[:, off:off + 512].rearrange("p (h q) -> p h q", h=4),
               bs_t[:, half * 4:(half + 1) * 4].unsqueeze(2).to_broadcast([128, 4, 128]), ALU.add, [dd, bs_t], [cmt])
        TT("dve", ct[:, 1024:2048], cmt[:], gu_[:], ALU.mult, [cmt, gu_], [cat_b])

    def FIN_full(seg, c):
        gc = seg["gc0"] + c
        DMA("sp", catscr[gc * 128:(gc + 1) * 128, :], cat[0][:], "catst", [cat_a, cat_b], [])

    def B_full(seg, c, par):
        gc = seg["gc0"] + c
        xs_, B_, BCT_, sz_ = xs_tm[par], B_tm[par], BCT[par], sz[par]
        DMA("sp", Hst1[:], hbscr[gc], "ldhb", [], [Hst1])
        dt_stuff([0, 1], par)
        dtA, nac, ea, dt, cd, wgt = st_["dtA"], st_["nac"], st_["ea"], st_["dt"], st_["cd"], st_["wgt"]
        CP("dve", Hbf[0][:], H[:], [H], [Hbf[0]])
        CP("dve", dtA2[:, :, 0:64].rearrange("p d (r c) -> p d r c", r=2), dtA[:].unsqueeze(2).to_broadcast([128, 2, 2, 32]), [dtA], [dtA2])
        for d_ in range(2):
            tri = tri_f if d_ == 0 else tri_r
            MM(pS2[:, 128 + d_ * 128: 128 + (d_ + 1) * 128], dtA2[:, d_, :], tri[:], True, True, [dtA2, tri], [dS2])
        acv = pS2[:, 128:384].rearrange("p (a b) -> p a b", a=2)
        CP("act", ACs[0:32], acv[0:32], [dS2], [ACs])
        CP("act", tmpb[32:64], acv[32:64], [dS2], [tmpb])
        TT("dve", ACs[32:64], acv[32:64], tmpb[32:64], ALU.subtract, [dS2, tmpb], [ACs])
        TS("dve", NACs[0:64], ACs[0:64], -1.0, None, ALU.mult, None, [ACs], [NACs])
        pt_, off, dd = next_half(1)
        for g in range(4):
            MM(pt_[:, off + g * 128: off + (g + 1) * 128], BCT_[:, g, :], BCT_[:, 4 + g, :], True, True, [BCT_], [dd])
        CP("act", cbT[:], pt_[:, off:off + 512].rearrange("p (a b) -> p a b", a=4), [dd], [cbT])

        grp = {}

        def dec_mm(g, d_):
            pt_, off, dd = next_half(1)
            wide = pt_[:, off:off + 512].rearrange("p (a b) -> p a b", a=4)
            MM(wide, NACs[:, d_, :], SEL2[:, g * 4:(g + 1) * 4, :], True, False, [SEL2, NACs], [dd])
            MM(wide, ident[:], negm[:, d_, :].unsqueeze(1).to_broadcast([128, 4, 128]), False, False, [ident, negm], [dd])
            for hh in range(4):
                h_ = g * 4 + hh
                o = pt_[:, off + hh * 128: off + (hh + 1) * 128]
                MM(o, SEL2[:, h_, :], ACs[:, d_, :], False, hh == 3, [SEL2, ACs], [dd])
            grp[(g, d_)] = (pt_, off, dd)

        def dec_ev(g, d_):
            pt_, off, dd = grp.pop((g, d_))
            ACT(decT[d_][:], pt_[:, off:off + 512].rearrange("p (a b) -> p a b", a=4), AF.Exp, [dd], [decT[d_]])
            TT("dve", MT[d_][:], decT[d_][:], cbT[:, g, :].unsqueeze(1).to_broadcast([128, 4, 128]), ALU.mult, [decT[d_], cbT], [MT[d_]])

        def y_group(g):
            for hh in range(4):
                h_ = g * 4 + hh
                for d_ in range(2):
                    MM(pY[:, h_ * 64:(h_ + 1) * 64], MT[d_][:, hh, :], xdt[d_][:, h_ * 64:(h_ + 1) * 64], d_ == 0, d_ == 1, [MT[d_], xdt[d_]], [pY])

        dec_mm(0, 0)
        dec_mm(0, 1)
        for d_ in range(2):
            TT("pool", xdt[d_][:].rearrange("p (h q) -> p h q", h=16), xs_[:].rearrange("p (h q) -> p h q", h=16),
               dt[:, d_, 0:16].unsqueeze(2).to_broadcast([128, 16, 64]), ALU.mult, [xs_, dt], [xdt[d_]])
        v3h = lambda ap: ap.rearrange("p (h q) -> p h q", h=8)
        bch = lambda t, d_, half: t[:, d_, half * 8:(half + 1) * 8].unsqueeze(2).to_broadcast([128, 8, 64])
        for half in range(2):
            hs = slice(half * 512, (half + 1) * 512)
            for d_, Hs in enumerate((Hbf[0], Hst1)):
                for gg in range(2):
                    g = half * 2 + gg
                    MM(pS2[:, gg * 256:(gg + 1) * 256], BCT_[:, 4 + g, :], Hs[:, g * 256:(g + 1) * 256], True, True, [BCT_, Hs], [dS2])
                if d_ == 0:
                    TT("dve", v3h(ysb[:, hs]), v3h(pS2[:, 0:512]), bch(ea, 0, half), ALU.mult, [dS2, ea], [ysb])
                else:
                    TT("dve", v3h(t1[:, hs]), v3h(pS2[:, 0:512]), bch(ea, 1, half), ALU.mult, [dS2, ea], [t1])
                    TT("dve", ysb[:, hs], ysb[:, hs], t1[:, hs], ALU.add, [ysb, t1], [ysb])
        TT("pool", t1[:].rearrange("p (h q) -> p h q", h=16), xs_[:].rearrange("p (h q) -> p h q", h=16),
           dskip_b[:].unsqueeze(2).to_broadcast([128, 16, 64]), ALU.mult, [xs_, dskip_b], [t1])
        TT("dve", ysb[:], ysb[:], t1[:], ALU.add, [ysb, t1], [ysb])
        xw_ = Hbf[0]
        TT("dve", xw_[:].rearrange("p (h q) -> p h q", h=16), xs_[:].rearrange("p (h q) -> p h q", h=16),
           wgt[:, 0, 0:16].unsqueeze(2).to_broadcast([128, 16, 64]), ALU.mult, [xs_, wgt], [xw_])
        for half in range(2):
            hs = slice(half * 512, (half + 1) * 512)
            for gg in range(2):
                g = half * 2 + gg
                MM(pS2[:, gg * 256:(gg + 1) * 256], B_[:, g * 128:(g + 1) * 128], xw_[:, g * 256:(g + 1) * 256], True, True, [B_, xw_], [dS2])
            TT("dve", v3h(H[:, hs]), v3h(H[:, hs]), bch(cd, 0, half), ALU.mult, [H, cd], [H])
            TT("dve", H[:, hs], H[:, hs], pS2[:, 0:512], ALU.add, [H, dS2], [H])
        for g in range(4):
            dec_ev(g, 0)
            if g + 1 < 4:
                dec_mm(g + 1, 0)
            dec_ev(g, 1)
            if g + 1 < 4:
                dec_mm(g + 1, 1)
            y_group(g)
        TT("dve", ysb[:], ysb[:], pY[:], ALU.add, [ysb, pY], [ysb])
        TT("dve", ysb[:], ysb[:], sz_[:], ALU.mult, [ysb, sz_], [ysb])
        rms_rstd(ysb[:], ysb, ssB, 1, t1[:], t1)
        STT(cat[0][:, 0:1024], ysb[:], ssB[:, 1:2], ssdg_b[:], ALU.mult, ALU.mult, [ysb, ssB, ssdg_b], [cat_a])

    for seg in segs:
        ensure_mod1(seg["ci"])
        h_init(seg, 0)
        own = seg["own"]
        nfe = own if seg["kind"] == "prompt" else own + 1
        run_pass(seg, list(range(nfe)), list(range(own)), PROJ_full, B_full, CM_full, FIN_full)
        if seg["kind"] == "prompt":
            h_out(seg["seq"], 0)
    cat_done = [("catst", S.dma_count["catst"], "dma")]

    def barrier_tokens():
        toks = [(e_, S.count[e_], e_) for e_ in S.ENGS if S.count[e_] > 0]
        toks += [(k_, v_, "dma") for k_, v_ in S.dma_count.items()]
        return toks

    def guard(tiles, toks):
        for t_ in tiles:
            t_.d.readers.extend(toks)

    bar1 = barrier_tokens()
    A.reset(m_core)
    Wout = sbt("Wout", [128, 16, D], BF16)
    g1b = [sbt(f"g1b{i}", [128, D]) for i in range(2)]
    xt2 = [sbt(f"x2t{i}", [128, D]) for i in range(2)]
    posL2 = [sbt(f"posL2{i}", [128, 512]) for i in range(2)]
    catl = [sbt(f"catl{i}", [128, 2048], BF16) for i in range(2)]
    catT = sbt("catT", [128, 16, 128], BF16)
    x1t = [sbt(f"x1t{i}", [128, D]) for i in range(2)]
    W_BASE = Arena.END - 131072
    assert A.cur <= W_BASE, f"stage2 working set overlaps FFN weights: {A.cur} > {W_BASE}"
    A.cur = W_BASE
    W1 = sbt("W1", [128, 8, 4096], BF16)
    W2 = sbt("W2", [128, 32, D], BF16)
    guard([Wout] + g1b + xt2 + posL2 + catl + [catT] + x1t + [W1, W2], bar1)
    wout_v = wout_d.rearrange("(kt p) n -> p kt n", p=128)
    w1_v = wff1_d.rearrange("(kt p) n -> p kt n", p=128)
    w2_v = wff2_d.rearrange("(kt p) n -> p kt n", p=128)
    dWout = [Dep(f"Wout{i}") for i in range(4)]
    dW1 = [Dep(f"W1_{i}") for i in range(8)]
    dW2 = [Dep(f"W2_{i}") for i in range(8)]
    for d_ in dWout + dW1 + dW2:
        d_.readers.extend(bar1)
    for kt4 in range(4):
        DMA("pool", Wout[:, kt4 * 4:(kt4 + 1) * 4, :], wout_v[:, kt4 * 4:(kt4 + 1) * 4, :], f"w_out{kt4}", [], [dWout[kt4]])
    S.wait_all("pool", [(f"w_out{i}", 16, "dma") for i in range(4)])
    for kt in range(8):
        DMA("pool", W1[:, kt, :], w1_v[:, kt, :], f"w_ff1_{kt}", [], [dW1[kt]])
    for ci in range(2):
        DMA("sp", g1b[ci][:], modscr[ci, :, 2048:3072], f"c_g1b{ci}", [], [g1b[ci]])
    S.wait_all("sp", cat_done)
    xt = xt2
    posL = posL2

    def tile_src(gc):
        if gc < 4:
            return xp_d[gc * 128:(gc + 1) * 128, :], None, 0
        c = gc - 4
        return xs_d[c * 128:(c + 1) * 128, :], c, 1

    pend = {}

    def s2_load(gc):
        if gc < 20:
            src, pc, ci = tile_src(gc)
            xtile = xt[gc % 2]
            pl = load_x(xtile, src, pc)
            ctl = catl[gc % 2]
            DMA("sp", ctl[:], catscr[gc * 128:(gc + 1) * 128, :], f"ld_{ctl.d.name}", [], [ctl])
            pend[gc] = (xtile, pl, ctl, ci)

    s2_load(0)
    for gc in range(20):
        s2_load(gc + 1)
        xtile, pl, ctl, ci = pend.pop(gc)
        add_pos(xtile, pl)
        for half in range(2):
            for a in range(8):
                kt = half * 8 + a
                TR(pT[:, a * 128:(a + 1) * 128], ctl[:, kt * 128:(kt + 1) * 128], ident[:], [ctl, ident], [pT])
            CP("act", catT[:, half * 8:(half + 1) * 8, :], pT[:].rearrange("p (a b) -> p a b", a=8), [pT], [catT])
        pw = pW0 if gc % 2 == 0 else pW1
        hd_ = W0D if gc % 2 == 0 else W1D
        for nb in range(2):
            for kt in range(16):
                MM(pw[:, nb * 512:(nb + 1) * 512], catT[:, kt, :], Wout[:, kt, nb * 512:(nb + 1) * 512], kt == 0, kt == 15, [catT, dWout[kt // 4]], [hd_[nb]])
        x1 = x1t[gc % 2]
        TT("dve", x1[:], pw[:], g1b[ci][:], ALU.mult, hd_ + [g1b[ci]], [x1])
        TT("pool", x1[:], x1[:], xtile[:], ALU.add, [x1, xtile], [x1])
        DMA("sp", x1scr[gc * 128:(gc + 1) * 128, :], x1[:], f"st_{x1.d.name}", [x1], [])
    for k4 in range(8):
        DMA("pool", W2[:, k4 * 4:(k4 + 1) * 4, :], w2_v[:, k4 * 4:(k4 + 1) * 4, :], f"w_ff2_{k4}", [], [dW2[k4]])
    x1_done = [(f"st_x1t{i}", S.dma_count[f"st_x1t{i}"], "dma") for i in range(2)]

    bar2 = barrier_tokens()
    A.reset(m_core)
    G2 = sbt("G2", [128, D])
    sh2 = sbt("sh2", [128, D])
    g2b = sbt("g2b", [128, D])
    fg_b = sbt("fg_b", [128, D])
    x1l = [sbt(f"x1l{i}", [128, 2, D]) for i in range(2)]
    h2 = sbt("h2", [128, D], BF16)
    h2T = sbt("h2T", [128, 8, 256], BF16)
    aT = sbt("aT", [128, 32, 256], BF16)
    rl = [sbt(f"rl{i}", [128, 256], BF16) for i in range(2)]
    yo = [sbt(f"yo{i}", [128, D]) for i in range(2)]
    junk3 = sbt("junk3", [128, D], BF16)
    ss3 = sbt("ss3", [128, 4])
    tmp3 = sbt("tmp3", [128, D])
    assert A.cur <= W_BASE, f"stage3 working set overlaps FFN weights: {A.cur} > {W_BASE}"
    guard([G2, sh2, g2b, fg_b] + x1l + [h2, h2T, aT] + rl + yo + [junk3, ss3, tmp3], bar2)
    DMA("sp", fg_b[:], vecs_d[2:3, :].partition_broadcast(128), "c_fg_b", [], [fg_b])
    S.wait_all("sp", x1_done)

    h2Tb = [h2T, sbt("h2T1", [128, 8, 256], BF16)]
    ss3p = sbt("ss3p", [128, 4])
    guard([h2Tb[1], ss3p], bar2)
    assert A.cur <= W_BASE, f"stage3 working set overlaps FFN weights: {A.cur} > {W_BASE}"

    def load_G2sh2(ci):
        DMA("sp", tmp3[:], vecs_d[1:2, :].partition_broadcast(128), "c_tmp3", [], [tmp3])
        DMA("sp", sh2[:], modscr[ci, :, 3072:4096], "c_sh2", [], [sh2])
        DMA("sp", G2[:], modscr[ci, :, 4096:5120], "c_G2", [], [G2])
        STT(G2[:], G2[:], 1.0, tmp3[:], ALU.add, ALU.mult, [G2, tmp3], [G2])

    def load_g2b(ci):
        DMA("sp", g2b[:], modscr[ci, :, 5120:6144], "c_g2b", [], [g2b])

    def rms3(src_ap, src_t, sst, col, dump_ap, dump_t):
        ACT(dump_ap, src_ap, AF.Square, [src_t], [dump_t, sst], accum_out=sst[:, col:col + 1])
        ACT(sst[:, col:col + 1], sst[:, col:col + 1], AF.Ln, [sst], [sst], scale=1.0 / D, bias=eps_t[:, 0:1])
        ACT(sst[:, col:col + 1], sst[:, col:col + 1], AF.Exp, [sst], [sst], scale=-0.5)

    def s3_load(b):
        if b < 10:
            xl = x1l[b % 2]
            DMA("sp", xl[:], x1scr[b * 256:(b + 1) * 256, :].rearrange("(a p) n -> p a n", p=128), f"ld_{xl.d.name}", [], [xl])

    def prep(b):
        xl = x1l[b % 2]
        hT = h2Tb[b % 2]
        for t_i in range(2):
            rms3(xl[:, t_i, :], xl, ss3p, 0, h2[:], h2)
            STT(h2[:], xl[:, t_i, :], ss3p[:, 0:1], G2[:], ALU.mult, ALU.mult, [xl, ss3p, G2], [h2])
            TT("pool", h2[:], h2[:], sh2[:], ALU.add, [h2, sh2], [h2])
            for kt in range(8):
                TR(pT[:, kt * 128:(kt + 1) * 128], h2[:, kt * 128:(kt + 1) * 128], ident[:], [h2, ident], [pT])
            CP("act", hT[:, :, t_i * 128:(t_i + 1) * 128], pT[:].rearrange("p (a b) -> p a b", a=8), [pT], [hT])

    yo_ctr = [0]

    def ffn(b):
        xl = x1l[b % 2]
        hT = h2Tb[b % 2]
        for ft in range(32):
            pt_, off, dd = halves[2 + ft % 2]
            for kt in range(8):
                MM(pt_[:, off:off + 256], W1[:, kt, ft * 128:(ft + 1) * 128], hT[:, kt, :], kt == 0, kt == 7, [dW1[kt], hT], [dd])
            r_ = rl[ft % 2]
            ACT(r_[:], pt_[:, off:off + 256], AF.Relu, [dd], [r_])
            TT("pool" if ft % 2 == 0 else "dve", aT[:, ft, :], r_[:], r_[:], ALU.mult, [r_], [aT])
        for t_i in range(2):
            for nb in range(2):
                dd = W0D[nb] if t_i == 0 else pY.d
                o = pW0[:, nb * 512:(nb + 1) * 512] if t_i == 0 else pY[:, nb * 512:(nb + 1) * 512]
                for ft in range(32):
                    MM(o, aT[:, ft, t_i * 128:(t_i + 1) * 128], W2[:, ft, nb * 512:(nb + 1) * 512], ft == 0, ft == 31, [aT, dW2[ft // 4]], [dd])
            src = pW0 if t_i == 0 else pY
            rd = W0D if t_i == 0 else [pY.d]
            y_ = yo[yo_ctr[0] % 2]
            yo_ctr[0] += 1
            TT("dve", tmp3[:], src[:], g2b[:], ALU.mult, rd + [g2b], [tmp3])
            TT("dve", tmp3[:], tmp3[:], xl[:, t_i, :], ALU.add, [tmp3, xl], [tmp3])
            rms3(tmp3[:], tmp3, ss3, 1, junk3[:], junk3)
            STT(y_[:], tmp3[:], ss3[:, 1:2], fg_b[:], ALU.mult, ALU.mult, [tmp3, ss3, fg_b], [y_])
            gc = b * 2 + t_i
            if gc < 4:
                dst = yp_d[gc * 128:(gc + 1) * 128, :]
            else:
                dst = ys_d[(gc - 4) * 128:(gc - 3) * 128, :]
            DMA("sp", dst, y_[:], f"st_{y_.d.name}", [y_], [])

    s3_load(0)
    s3_load(1)
    load_G2sh2(0)
    load_g2b(0)
    prep(0)
    for b in range(10):
        if b + 1 == 2:
            load_G2sh2(1)
        ths = [record(lambda: ffn(b))]
        if b + 1 < 10:
            ths.append(record(lambda: prep(b + 1)))
        merge(ths, [0.0, 0.25])
        if b + 1 == 2:
            load_g2b(1)
        s3_load(b + 2)

    S.wait_all("sp", [(k_, v_, "dma") for k_, v_ in S.dma_count.items()])
    S.emit()
    return nc


_NC_CACHE = {}


def _prep_inputs(inp):
    f = lambda a: np.ascontiguousarray(np.asarray(a, dtype=np.float32))
    x_prompt, x_sample, state = f(inp["x_prompt"]), f(inp["x_sample"]), f(inp["state_ssd"])
    c, c_ctx = f(inp["c"]), f(inp["c_ctx"])
    w_in = f(inp["w_in"])[0]
    conv_w, conv_b = f(inp["conv_w"])[0], f(inp["conv_b"])[0]
    dt_bias, A_log = f(inp["dt_bias"])[0], f(inp["A_log"])[0]
    cm_w_s, cm_b_s = f(inp["cm_w_s"])[0], f(inp["cm_b_s"])[0]
    vecs = np.stack([f(inp["norm1_g"])[0], f(inp["norm2_g"])[0], f(inp["final_norm_g"]), f(inp["ssd_norm_g"])[0],
                     f(inp["cm_ln_g"])[0], f(inp["cm_ln_b"])[0]], axis=0)
    shared = {
        "w_ada": f(inp["w_ada"])[0], "b_ada": f(inp["b_ada"])[0][None, :], "vecs": f(vecs),
        "conv_b": conv_b[None, :], "conv_bT": f(conv_b.reshape(16, 128).T), "dskip": f(inp["d_skip"])[0][None, :], "w_out": f(inp["w_out"])[0],
        "w_ff1": f(inp["w_ff1"])[0], "w_ff2": f(inp["w_ff2"])[0],
    }
    variants = []
    for mir in (False, True):
        v = dict(shared)
        if not mir:
            v["w_in"] = w_in
            cw = conv_w
            dtb, alog = dt_bias, A_log
            ws, bsv = cm_w_s, cm_b_s
            v["rowpos"] = np.arange(64, dtype=np.float32)[:, None]
            v["colpos"] = f((np.arange(128) % 64).astype(np.float32)[:, None])
        else:
            wi = w_in.copy()
            wi[:, C_DT:C_DT + 16] = w_in[:, C_DT + 16:C_DT + 32]
            wi[:, C_DT + 16:C_DT + 32] = w_in[:, C_DT:C_DT + 16]
            v["w_in"] = wi
            cw = conv_w[::-1]
            dtb, alog = dt_bias[::-1], A_log[::-1]
            ws, bsv = cm_w_s[:, ::-1, ::-1], cm_b_s[:, ::-1]
            v["rowpos"] = f(np.arange(63, -1, -1, dtype=np.float32)[:, None])
            v["colpos"] = f((63 - (np.arange(128) % 64)).astype(np.float32)[:, None])
        v["conv_wT"] = f(cw.reshape(5, 16, 128).transpose(2, 1, 0).reshape(128, 80))
        v["dtb"] = f(dtb.reshape(1, 32))
        v["alog"] = f(alog.reshape(1, 32))
        v["wsT"] = f(ws.transpose(2, 0, 1).reshape(128, 1024))
        v["bs"] = f(bsv.T)
        variants.append(v)
    in_maps = []
    for k in range(8):
        b, mir = k // 2, (k % 2 == 1)
        m = dict(variants[1 if mir else 0])
        xs = x_sample[b]
        xp = x_prompt[2 * k:2 * k + 2]
        st = state[b, 0]
        if mir:
            xs = xs[::-1]
            xp = xp[:, ::-1]
            st = st[::-1]
        m["xs"] = f(xs)
        m["xp"] = f(xp.reshape(512, D))
        m["st_in"] = f(st.reshape(2, 1024, 128))
        cond = np.stack([c_ctx, c[b]], axis=0)
        m["condT"] = f(cond.reshape(2, 8, 128).transpose(2, 0, 1).reshape(128, 16))
        in_maps.append(m)
    return in_maps


def kernel(**inputs):
    if "nc" not in _NC_CACHE:
        _NC_CACHE["nc"] = build_program()
    nc = _NC_CACHE["nc"]
    in_maps = _prep_inputs(inputs)
    res = run_bass_kernel_spmd(nc, in_maps, core_ids=list(range(8)))
    y_prompt = np.zeros((16, 256, D), np.float32)
    y_sample = np.zeros((4, 4096, D), np.float32)
    new_state = np.zeros((16, 1, 2, 16, 64, 128), np.float32)
    for k in range(8):
        r = res.results[k]
        b, mir = k // 2, (k % 2 == 1)
        yp = np.asarray(r["yp"]).reshape(2, 256, D)
        ys = np.asarray(r["ys"])
        ns = np.asarray(r["nst"]).reshape(2, 2, 16, 64, 128)
        if mir:
            yp = yp[:, ::-1]
            ys = ys[::-1]
            ns = ns[:, ::-1]
            y_sample[b, 2048:] = ys
        else:
            y_sample[b, :2048] = ys
        y_prompt[2 * k:2 * k + 2] = yp
        new_state[2 * k:2 * k + 2, 0] = ns
    return (y_prompt, y_sample, new_state)
```

```python
import contextlib
import math
import numpy as np
import concourse.bass as bass
import concourse.mybir as mybir
from concourse.bass_utils import run_bass_kernel_spmd

F32 = mybir.dt.float32
BF16 = mybir.dt.bfloat16
I32 = mybir.dt.int32
AF = mybir.ActivationFunctionType
ALU = mybir.AluOpType

D = 1024
NIN = 5152
EPS = 1e-6
C_Z, C_X, C_B, C_C, C_DT, C_U, C_V = 0, 1024, 2048, 2560, 3072, 3104, 4128
NEG = -30000.0


class Dep:
    __slots__ = ("name", "writer", "readers")

    def __init__(self, name):
        self.name = name
        self.writer = None
        self.readers = []


class Sched:
    ENGS = ("pe", "act", "dve", "pool", "sp")

    def __init__(self, nc):
        self.nc = nc
        self.ops = {e: [] for e in self.ENGS}
        self.count = {e: 0 for e in self.ENGS}
        self.dma_count = {}
        self.waited = {e: {} for e in self.ENGS}
        self.sem_keys = list(self.ENGS)
        self.needed = {e: set() for e in self.ENGS}

    def _need(self, eng, tok, waits):
        if tok is None:
            return
        key, val, src = tok
        if src == "pe" and eng == "pe":
            return
        cur = self.waited[eng].get(key, 0)
        if cur >= val:
            return
        self.waited[eng][key] = val
        if src != "dma":
            self.needed[key].add(val)
        waits.append((key, val, src))

    def op(self, eng, fn, reads=(), writes=(), dma=None):
        waits = []
        for d in reads:
            self._need(eng, d.writer, waits)
        for d in writes:
            self._need(eng, d.writer, waits)
            for r in d.readers:
                self._need(eng, r, waits)
        if dma is None:
            self.count[eng] += 1
            tok = (eng, self.count[eng], eng)
            inc = (eng, 1)
        else:
            if dma not in self.dma_count:
                self.dma_count[dma] = 0
                self.sem_keys.append(dma)
            self.dma_count[dma] += 16
            tok = (dma, self.dma_count[dma], "dma")
            inc = (dma, 16)
        for d in reads:
            d.readers.append(tok)
        for d in writes:
            d.writer = tok
            d.readers = []
        self.ops[eng].append((waits, fn, inc, tok[1]))
        return tok

    def wait_all(self, eng, toks):
        waits = []
        for t in toks:
            self._need(eng, t, waits)
        self.ops[eng].append((waits, None, None, None))

    def emit(self):
        nc = self.nc
        import bisect
        ranks = {e: sorted(self.needed[e]) for e in self.ENGS}

        def wval(key, val, src):
            if src == "dma":
                return val
            return bisect.bisect_right(ranks[key], val)

        with contextlib.ExitStack() as st:
            sems = {}
            for k in self.sem_keys:
                sems[k] = st.enter_context(nc.semaphore("s_" + k))
            block = st.enter_context(nc.Block())

            def run(engname, handle):
                for waits, fn, inc, seq in self.ops[engname]:
                    for (k, v, src) in waits:
                        handle.wait_ge(sems[k], wval(k, v, src))
                    if fn is not None:
                        ins = fn(handle)
                        if inc[1] == 16 or seq in self.needed[engname]:
                            ins.then_inc(sems[inc[0]], inc[1])

            @block.tensor
            def _(e):
                run("pe", e)

            @block.scalar
            def _(e):
                run("act", e)

            @block.vector
            def _(e):
                run("dve", e)

            @block.gpsimd
            def _(e):
                run("pool", e)

            @block.sync
            def _(e):
                run("sp", e)


class Arena:
    BASE = 16640
    END = 229312

    def __init__(self, nc):
        self.nc = nc
        self.cur = self.BASE
        self.n = 0

    def alloc(self, name, shape, dtype):
        esz = 4 if dtype in (F32, I32) else 2
        size = esz * int(np.prod(shape[1:]))
        off = (self.cur + 31) // 32 * 32
        assert off + size <= self.END, f"SBUF overflow allocating {name}: {off}+{size}"
        self.cur = off + size
        self.n += 1
        return self.nc.alloc_sbuf_tensor_at(f"{name}_{self.n}", list(shape), dtype, offset=off)

    def mark(self):
        return self.cur

    def reset(self, m):
        self.cur = m


class T:
    def __init__(self, h, name):
        self.h = h
        self.d = Dep(name)

    def __getitem__(self, k):
        return self.h[k]


def build_program():
    nc = bass.Bass("TRN2", target_bir_lowering=False)
    S = Sched(nc)
    A = Arena(nc)

    def din(name, shape, dt=F32):
        return nc.dram_tensor(name, list(shape), dt, kind="ExternalInput").ap()

    def dout(name, shape, dt=F32):
        return nc.dram_tensor(name, list(shape), dt, kind="ExternalOutput").ap()

    def dscr(name, shape, dt=F32):
        return nc.dram_tensor(name, list(shape), dt, kind="Internal").ap()

    xs_d = din("xs", [4096, D])
    xp_d = din("xp", [512, D])
    condT_d = din("condT", [128, 16])
    st_d = din("st_in", [2, 1024, 128])
    wada_d = din("w_ada", [D, 6144])
    bada_d = din("b_ada", [1, 6144])
    vecs_d = din("vecs", [6, D])
    win_d = din("w_in", [D, NIN])
    convw_d = din("conv_wT", [128, 80])
    convb_d = din("conv_b", [1, 2048])
    convbT_d = din("conv_bT", [128, 16])
    dtb_d = din("dtb", [1, 32])
    alog_d = din("alog", [1, 32])
    dskip_d = din("dskip", [1, 16])
    wsT_d = din("wsT", [128, 1024])
    bs_d = din("bs", [128, 8])
    wout_d = din("w_out", [2048, D])
    wff1_d = din("w_ff1", [D, 4096])
    wff2_d = din("w_ff2", [4096, D])
    rowpos_d = din("rowpos", [64, 1])
    colpos_d = din("colpos", [128, 1])

    yp_d = dout("yp", [512, D])
    ys_d = dout("ys", [2048, D])
    nst_d = dout("nst", [2, 2, 1024, 128])

    modscr = dscr("modscr", [2, 128, 6144])
    hbscr = dscr("hbscr", [20, 128, 1024], BF16)
    catscr = dscr("catscr", [2560, 2048], BF16)
    x1scr = dscr("x1scr", [2560, D])
    rtab = dscr("rtab", [64, 512])
    xsscr = dscr("xsscr", [20, 128, 1024], BF16)
    bscr = dscr("bscr", [20, 128, 512], BF16)

    out_toks = []

    def deps(xs):
        return [x.d if isinstance(x, T) else x for x in xs]

    REC = [None]

    def emit(eng, fn, reads, writes, dma=None, tag=None):
        if REC[0] is not None:
            REC[0].append((eng, fn, list(reads), list(writes), dma, tag))
            return None
        return S.op(eng, fn, reads, writes, dma=dma)

    def record(f):
        assert REC[0] is None
        REC[0] = []
        try:
            f()
            return REC[0]
        finally:
            REC[0] = None

    cur_set = [None]
    SLACK = 0.12

    def merge(lists, bias=None):
        if bias is None:
            bias = [0.0] * len(lists)
        keep = [j for j, l in enumerate(lists) if l]
        bias = [bias[j] for j in keep]
        lists = [lists[j] for j in keep]
        idx = [0] * len(lists)
        while True:
            cands = []
            for j, l in enumerate(lists):
                if idx[j] < len(l):
                    cands.append((idx[j] / len(l) + bias[j], j))
            if not cands:
                break
            cands.sort()
            pick = cands[0][1]
            tag0 = lists[pick][idx[pick]][5]
            if tag0 is not None and cur_set[0] is not None and tag0 != cur_set[0]:
                for fr, j in cands[1:]:
                    tg = lists[j][idx[j]][5]
                    if fr <= cands[0][0] + SLACK and (tg is None or tg == cur_set[0]):
                        pick = j
                        break
            eng, fn, reads, writes, dma, tag = lists[pick][idx[pick]]
            idx[pick] += 1
            if tag is not None:
                cur_set[0] = tag
            S.op(eng, fn, reads, writes, dma=dma)

    def MM(out, lhsT, rhs, start, stop, reads, writes):
        emit("pe", lambda e: e.matmul(out, lhsT=lhsT, rhs=rhs, start=start, stop=stop), deps(reads), deps(writes))

    def TR(out, in_, ident_ap, reads, writes):
        emit("pe", lambda e: e.transpose(out=out, in_=in_, identity=ident_ap), deps(reads), deps(writes))

    ACT_TAGS = {AF.Silu: "silu", AF.Gelu_apprx_tanh: "gelu", AF.Exp: "expln", AF.Ln: "expln", AF.Sin: "sin", AF.Sqrt: "sqrt"}

    def ACT(out, in_, func, reads, writes, **kw):
        emit("act", lambda e: e.activation(out=out, in_=in_, func=func, **kw), deps(reads), deps(writes), tag=ACT_TAGS.get(func))

    def TT(eng, out, in0, in1, op, reads, writes):
        emit(eng, lambda e: e.tensor_tensor(out=out, in0=in0, in1=in1, op=op), deps(reads), deps(writes))

    def TS(eng, out, in0, s1, s2, op0, op1, reads, writes):
        if s2 is None:
            emit(eng, lambda e: e.tensor_scalar(out=out, in0=in0, scalar1=s1, scalar2=None, op0=op0), deps(reads), deps(writes))
        else:
            emit(eng, lambda e: e.tensor_scalar(out=out, in0=in0, scalar1=s1, scalar2=s2, op0=op0, op1=op1), deps(reads), deps(writes))

    def STT(out, in0, scalar, in1, op0, op1, reads, writes):
        emit("dve", lambda e: e.scalar_tensor_tensor(out=out, in0=in0, scalar=scalar, in1=in1, op0=op0, op1=op1), deps(reads), deps(writes))

    def CP(eng, out, in_, reads, writes):
        if eng == "act":
            emit("act", lambda e: e.copy(out=out, in_=in_), deps(reads), deps(writes))
        else:
            emit(eng, lambda e: e.tensor_copy(out=out, in_=in_), deps(reads), deps(writes))

    def MSET(eng, ap, val, writes):
        emit(eng, lambda e: e.memset(ap, val), [], deps(writes))

    def DMA(eng, out, in_, key, reads, writes):
        return emit(eng, lambda e: e.dma_start(out=out, in_=in_), deps(reads), deps(writes), dma=key)

    def RECIP(out, in_, reads, writes):
        emit("dve", lambda e: e.reciprocal(out=out, in_=in_), deps(reads), deps(writes))

    def ASEL(out, in_, pattern, op, fill, base, cm, reads, writes):
        emit("pool", lambda e: e.affine_select(out=out, in_=in_, pattern=pattern, compare_op=op, fill=fill,
                                               base=base, channel_multiplier=cm), deps(reads), deps(writes))

    def sbt(name, shape, dt=F32):
        return T(A.alloc(name, shape, dt), name)

    pW0h = nc.alloc_psum_tensor("pW0", [128, 1024], F32)
    pW1h = nc.alloc_psum_tensor("pW1", [128, 1024], F32)
    pYh = nc.alloc_psum_tensor("pY", [128, 1024], F32)
    pTh = nc.alloc_psum_tensor("pT", [128, 1024], BF16)
    pSh = nc.alloc_psum_tensor("pS", [128, 512], F32)
    pW0, pW1, pY, pT, pS = T(pW0h, "pW0"), T(pW1h, "pW1"), T(pYh, "pY"), T(pTh, "pT"), T(pSh, "pS")
    halves = [(pW0, 0, Dep("pW0a")), (pW0, 512, Dep("pW0b")), (pW1, 0, Dep("pW1a")), (pW1, 512, Dep("pW1b"))]
    hctr = [0, 0, 0]

    def next_half(pair=2):
        if pair == 2:
            h = halves[hctr[2] % 4]
        else:
            h = halves[pair * 2 + hctr[pair] % 2]
        hctr[pair] += 1
        return h

    identf = sbt("identf", [128, 128])
    ident = sbt("ident", [128, 128], BF16)
    onesf = sbt("onesf", [128, 128])
    eps_t = sbt("eps_t", [128, 1])
    PC = sbt("PC", [128, 512])
    m_core = A.mark()
    tri_f = sbt("tri_f", [128, 128])
    tri_r = sbt("tri_r", [128, 128])
    negm = sbt("negm", [128, 2, 128], BF16)
    SEL2 = sbt("SEL2", [128, 16, 128], BF16)
    A_b = sbt("A_b", [128, 2, 32])
    dtb_b = sbt("dtb_b", [128, 2, 32])
    dskip_b = sbt("dskip_b", [128, 16])
    convb128 = sbt("convb128", [128, 128], BF16)
    convbT = sbt("convbT", [128, 16])
    convdiag = sbt("convdiag", [128, 80, 128], BF16)
    m_c1 = A.mark()
    zerosf = sbt("zerosf", [128, 128])
    negf = sbt("negf", [128, 2, 128])
    self32 = sbt("self32", [128, 16])
    convwT = sbt("convwT", [128, 80])

    MSET("pool", onesf[:], 1.0, [onesf])
    MSET("pool", zerosf[:], 0.0, [zerosf])
    MSET("pool", eps_t[:], EPS, [eps_t])
    ASEL(identf[:], onesf[:], [[-1, 128]], ALU.is_equal, 0.0, 0, 1, [onesf], [identf])
    CP("dve", ident[:], identf[:], [identf], [ident])
    ASEL(tri_f[:], onesf[:], [[1, 128]], ALU.is_ge, 0.0, 0, -1, [onesf], [tri_f])
    ASEL(tri_r[:], onesf[:], [[-1, 128]], ALU.is_ge, 0.0, 0, 1, [onesf], [tri_r])
    ASEL(negf[:, 0, :], zerosf[:], [[1, 128]], ALU.is_ge, NEG, 0, -1, [zerosf], [negf])
    ASEL(negf[:, 1, :], zerosf[:], [[-1, 128]], ALU.is_ge, NEG, 0, 1, [zerosf], [negf])
    CP("dve", negm[:], negf[:], [negf], [negm])
    MSET("pool", self32[:], 0.0, [self32])
    for blk in range(2):
        ASEL(self32[blk * 32:(blk + 1) * 32, :], onesf[blk * 32:(blk + 1) * 32, 0:16], [[-1, 16]], ALU.is_equal, 0.0, 0, 1, [onesf, self32], [self32])
    CP("dve", SEL2[:], self32[:].unsqueeze(2).to_broadcast([128, 16, 128]), [self32], [SEL2])
    MSET("pool", convb128[:], 0.0, [convb128])
    DMA("sp", convbT[:], convbT_d, "c_convbT", [], [convbT])
    DMA("pool", convb128[0:16, :], convb_d.rearrange("o (a b) -> (o a) b", a=16), "c_convb", [], [convb128])
    MSET("pool", A_b[:], 0.0, [A_b])
    MSET("pool", dtb_b[:], 0.0, [dtb_b])
    for d_ in range(2):
        DMA("sp", A_b[:, d_, 0:16], alog_d[:, d_ * 16:(d_ + 1) * 16].partition_broadcast(128), "c_A_b", [], [A_b])
        DMA("sp", dtb_b[:, d_, 0:16], dtb_d[:, d_ * 16:(d_ + 1) * 16].partition_broadcast(128), "c_dtb_b", [], [dtb_b])
    DMA("sp", dskip_b[:], dskip_d.partition_broadcast(128), "c_dskip", [], [dskip_b])
    ACT(A_b[:, :, 0:16], A_b[:, :, 0:16], AF.Exp, [A_b], [A_b])
    TS("dve", A_b[:, :, 0:16], A_b[:, :, 0:16], -1.0, None, ALU.mult, None, [A_b], [A_b])
    DMA("sp", convwT[:], convw_d, "c_convwT", [], [convwT])
    TT("dve", convdiag[:], identf[:].unsqueeze(1).to_broadcast([128, 80, 128]),
       convwT[:].unsqueeze(2).to_broadcast([128, 80, 128]), ALU.mult, [identf, convwT], [convdiag])

    m_persist = A.mark()
    d_setup_tmp = [zerosf, negf, self32, convwT]

    omega = sbt("omega", [128, 256])
    io_i = sbt("io_i", [128, 256], I32)
    S.op("pool", lambda e: e.iota(io_i[:], pattern=[[1, 256]], base=0, channel_multiplier=0), [], [io_i.d])
    CP("dve", omega[:], io_i[:], [io_i], [omega])
    ACT(omega[:], omega[:], AF.Exp, [omega], [omega], scale=-math.log(10000.0) / 256.0)
    pos_sb = sbt("pos_sb", [128, 1])
    ang = sbt("ang", [128, 512])
    ki = sbt("ki", [128, 512], I32)
    kf = sbt("kf", [128, 512])
    Rt = sbt("Rt", [128, 512])

    def sincos(dst_t, pos_dram, nrows):
        DMA("sp", pos_sb[0:nrows, :], pos_dram, "c_pos", [], [pos_sb])
        r = slice(0, nrows)
        TS("dve", ang[r, 0:256], omega[r, :], pos_sb[r, 0:1], 1.0 / (2 * math.pi), ALU.mult, ALU.mult, [omega, pos_sb], [ang])
        TS("dve", ang[r, 256:512], ang[r, 0:256], 0.25, None, ALU.add, None, [ang], [ang])
        CP("dve", ki[r, :], ang[r, :], [ang], [ki])
        CP("dve", kf[r, :], ki[r, :], [ki], [kf])
        TT("dve", ang[r, :], ang[r, :], kf[r, :], ALU.subtract, [ang, kf], [ang])
        TS("dve", ang[r, :], ang[r, :], -0.5, 0.5, ALU.max, ALU.min, [ang], [ang])
        ACT(dst_t[r, :], ang[r, :], AF.Sin, [ang], [dst_t], scale=6.28318)

    sincos(PC, colpos_d, 128)
    sincos(Rt, rowpos_d, 64)
    DMA("sp", rtab, Rt[0:64, :], "c_rtab", [Rt], [])
    d_rtab = Dep("rtab")
    d_rtab.writer = ("c_rtab", S.dma_count["c_rtab"], "dma")
    setup_tmp_toks = []
    for t_ in d_setup_tmp + [omega, io_i, pos_sb, ang, ki, kf, Rt]:
        if t_.d.writer is not None:
            setup_tmp_toks.append(t_.d.writer)
        setup_tmp_toks.extend(t_.d.readers)
    A.reset(m_c1)

    Win = sbt("Win", [128, 8, NIN], BF16)
    Win.d.readers.extend(setup_tmp_toks)
    win_v = win_d.rearrange("(kt p) n -> p kt n", p=128)
    def issue_win():
        for kt in range(8):
            DMA("pool", Win[:, kt, :], win_v[:, kt, :], "w_in", [], [Win])
    m_win = A.mark()

    condT = sbt("condT", [128, 16])
    condL = sbt("condL", [128, 16, 128], BF16)
    DMA("sp", condT[:], condT_d, "c_condT", [], [condT])
    ACT(condT[:], condT[:], AF.Silu, [condT], [condT])
    CP("dve", condL[:], condT[:].unsqueeze(2).to_broadcast([128, 16, 128]), [condT], [condL])
    NWB = 4
    wab = [sbt(f"wab{i}", [128, 8, 512], BF16) for i in range(NWB)]
    bab = [sbt(f"bab{i}", [128, 512]) for i in range(NWB)]
    mev = [sbt(f"mev{i}", [128, 512]) for i in range(4)]
    wada_v = wada_d.rearrange("(kt p) n -> p kt n", p=128)
    for blk in range(12):
        b2 = blk % NWB
        cs = slice(blk * 512, (blk + 1) * 512)
        DMA("pool", wab[b2][:], wada_v[:, :, cs], f"wada{b2}", [], [wab[b2]])
        DMA("sp", bab[b2][:], bada_d[:, cs].partition_broadcast(128), f"bada{b2}", [], [bab[b2]])
        if blk == NWB - 1:
            issue_win()
        for ci in range(2):
            pt_, off, dd = next_half()
            for kt in range(8):
                MM(pt_[:, off:off + 512], condL[:, ci * 8 + kt, :], wab[b2][:, kt, :], kt == 0, kt == 7, [condL, wab[b2]], [dd])
            ev = mev[(blk * 2 + ci) % 4]
            TT("dve", ev[:], pt_[:, off:off + 512], bab[b2][:], ALU.add, [dd, bab[b2]], [ev])
            DMA("sp", modscr[ci, :, cs], ev[:], f"modst{(blk * 2 + ci) % 4}", [ev], [])
    d_modscr = Dep("modscr")
    mod_toks = [(f"modst{i}", S.dma_count[f"modst{i}"], "dma") for i in range(4)]
    A.reset(m_win)
    bar0 = [(e_, S.count[e_], e_) for e_ in S.ENGS if S.count[e_] > 0] + [(k_, v_, 'dma') for k_, v_ in S.dma_count.items() if k_ != 'w_in']
    _sbt_plain = sbt

    def sbt(name, shape, dt=F32):
        t_ = _sbt_plain(name, shape, dt)
        t_.d.readers.extend(bar0)
        return t_

    G1 = sbt("G1", [128, D], BF16)
    sh1 = sbt("sh1", [128, D], BF16)
    t1 = sbt("t1", [128, D])
    ysb = sbt("ysb", [128, D])
    S.wait_all("sp", mod_toks)
    S.wait_all("pool", mod_toks)

    def load_mod1(ci):
        DMA("sp", t1[:], vecs_d[0:1, :].partition_broadcast(128), "c_t1", [], [t1])
        DMA("pool", sh1[:], modscr[ci, :, 0:1024], "c_sh1", [], [sh1])
        DMA("sp", ysb[:], modscr[ci, :, 1024:2048], "c_ysb", [], [ysb])
        STT(G1[:], ysb[:], 1.0, t1[:], ALU.add, ALU.mult, [ysb, t1], [G1])

    xt = [sbt(f"xt{i}", [128, D]) for i in range(2)]
    posL = [sbt(f"posL{i}", [128, 512]) for i in range(2)]
    hb = sbt("h", [128, D], BF16)
    hTx = [sbt(f"hTx{i}", [128, 8, 132], BF16) for i in range(3)]
    rawT = sbt("rawT", [128, 16, 132], BF16)
    xs_tm = [sbt(f"xs_tm{i}", [128, D], BF16) for i in range(2)]
    B_tm = [sbt(f"B_tm{i}", [128, 512], BF16) for i in range(2)]
    C_tm = sbt("C_tm", [128, 512], BF16)
    dtr2 = [sbt(f"dtr{i}", [128, 2, 32]) for i in range(2)]
    st_ = {n: sbt(n, [128, 2, 32]) for n in ("dt", "dtA", "nac", "ea", "cd", "dte", "wgt", "tmpd")}
    for t_ in list(st_.values()) + dtr2:
        MSET("pool", t_[:], 0.0, [t_])
    ss = sbt("ss", [128, 4])
    ssB = sbt("ssB", [128, 4])
    xdt = [sbt(f"xdt{i}", [128, D], BF16) for i in range(2)]
    xw = xdt[1]
    H = sbt("H", [128, D])
    Hbf = [sbt(f"Hbf{i}", [128, D], BF16) for i in range(2)]
    Hst1 = Hbf[1]
    stl = t1
    pos_ctr = [0]

    def load_x(xtile, src_rows, pos_chunk):
        DMA("sp", xtile[:], src_rows, f"ld_{xtile.d.name}", [], [xtile])
        if pos_chunk is not None:
            pl = posL[pos_ctr[0] % len(posL)]
            pos_ctr[0] += 1
            for e in range(2):
                DMA("sp", pl[e * 64:(e + 1) * 64, :], rtab[2 * pos_chunk + e:2 * pos_chunk + e + 1, :].partition_broadcast(64),
                    f"ld_{pl.d.name}", [d_rtab], [pl])
            return pl
        return None

    def add_pos(xtile, pl):
        if pl is not None:
            TT("pool", xtile[:, 0:512], xtile[:, 0:512], pl[:], ALU.add, [xtile, pl], [xtile])
            TT("pool", xtile[:, 512:1024], xtile[:, 512:1024], PC[:], ALU.add, [xtile, PC], [xtile])

    def rms_rstd(src_ap, src_t, sst, col, dump_ap, dump_t):
        ACT(dump_ap, src_ap, AF.Square, [src_t], [dump_t, sst], accum_out=sst[:, col:col + 1])
        ACT(sst[:, col:col + 1], sst[:, col:col + 1], AF.Ln, [sst], [sst], scale=1.0 / D, bias=eps_t[:, 0:1])
        ACT(sst[:, col:col + 1], sst[:, col:col + 1], AF.Exp, [sst], [sst], scale=-0.5)

    def FE(xtile, pl, slot):
        add_pos(xtile, pl)
        rms_rstd(xtile[:], xtile, ss, 0, hb[:], hb)
        STT(xtile[:], xtile[:], ss[:, 0:1], G1[:], ALU.mult, ALU.mult, [xtile, ss, G1], [xtile])
        TT("dve", hb[:], xtile[:], sh1[:], ALU.add, [xtile, sh1], [hb])
        for kt in range(8):
            TR(pT[:, kt * 128:(kt + 1) * 128], hb[:, kt * 128:(kt + 1) * 128], ident[:], [hb, ident], [pT])
        CP("act", hTx[slot][:, :, 2:130], pT[:].rearrange("p (a b) -> p a b", a=8), [pT], [hTx[slot]])

    def halo_copy(dst_slot, dst_lo, src_slot, src_lo):
        CP("pool", hTx[dst_slot][:, :, dst_lo:dst_lo + 2], hTx[src_slot][:, :, src_lo:src_lo + 2], [hTx[src_slot]], [hTx[dst_slot]])

    def halo_zero(slot, lo):
        MSET("pool", hTx[slot][:, :, lo:lo + 2], 0.0, [hTx[slot]])

    def raw_only(slot, ftiles):
        hx = hTx[slot]
        for g0 in range(0, len(ftiles), 3):
            grp = ftiles[g0:g0 + 3]
            pt_, off, dd = next_half(0)
            for j, ft in enumerate(grp):
                for kt in range(8):
                    MM(pt_[:, off + j * 132: off + (j + 1) * 132], Win[:, kt, C_X + ft * 128: C_X + (ft + 1) * 128], hx[:, kt, :],
                       kt == 0, kt == 7, [Win, hx], [dd])
            n = len(grp)
            CP("act", rawT[:, grp[0]:grp[0] + n, :], pt_[:, off: off + n * 132].rearrange("p (a b) -> p a b", a=n), [dd], [rawT])

    def raw_and_conv(slot, ftiles, par):
        hx = hTx[slot]
        for g0 in range(0, len(ftiles), 3):
            grp = ftiles[g0:g0 + 3]
            pt_, off, dd = next_half(0)
            for j, ft in enumerate(grp):
                for kt in range(8):
                    MM(pt_[:, off + j * 132: off + (j + 1) * 132], Win[:, kt, C_X + ft * 128: C_X + (ft + 1) * 128], hx[:, kt, :],
                       kt == 0, kt == 7, [Win, hx], [dd])
            n = len(grp)
            CP("act", rawT[:, grp[0]:grp[0] + n, :], pt_[:, off: off + n * 132].rearrange("p (a b) -> p a b", a=n), [dd], [rawT])
        for g0 in range(0, len(ftiles), 4):
            grp = ftiles[g0:g0 + 4]
            pt_, off, dd = next_half(0)
            for j, ft in enumerate(grp):
                o = pt_[:, off + j * 128: off + (j + 1) * 128]
                for k in range(5):
                    MM(o, rawT[:, ft, k:k + 128], convdiag[:, ft * 5 + k, :], k == 0, False, [rawT, convdiag], [dd])
                MM(o, SEL2[:, ft, :], convb128[:], False, True, [SEL2, convb128], [dd])
            ft0 = grp[0]
            if ft0 < 8:
                dst, dt_ = xs_tm[par][:, ft0 * 128:(ft0 + 4) * 128], xs_tm[par]
            elif ft0 < 12:
                dst, dt_ = B_tm[par][:], B_tm[par]
            else:
                dst, dt_ = C_tm[:], C_tm
            ACT(dst, pt_[:, off:off + 512], AF.Silu, [dd], [dt_])

    def dt_raw(slot, dirs, par):
        hx = hTx[slot]
        pt_, off, dd = next_half(0)
        ds = slice(dirs[0], dirs[-1] + 1)
        nd = len(dirs)
        c0 = C_DT + dirs[0] * 16
        for kt in range(8):
            MM(pt_[:, off: off + 16 * nd], hx[:, kt, 2:130], Win[:, kt, c0: c0 + 16 * nd], kt == 0, kt == 7, [hx, Win], [dd])
        TT("dve", dtr2[par][:, ds, 0:16], pt_[:, off:off + 16 * nd].rearrange("p (a b) -> p a b", a=nd), dtb_b[:, ds, 0:16], ALU.add, [dd, dtb_b], [dtr2[par]])

    def dt_stuff(dirs, par):
        ds = slice(dirs[0], dirs[-1] + 1)
        v = lambda t: t[:, ds, 0:16]
        pv = lambda c0: pS2[:, c0:c0 + 64].rearrange("p (a b) -> p a b", a=2)[:, ds, 0:16]
        dtr = dtr2[par]
        dt, dtA, nac, ea, cd, dte, wgt, tmpd = [st_[n] for n in ("dt", "dtA", "nac", "ea", "cd", "dte", "wgt", "tmpd")]
        ACT(v(dtr), v(dtr), AF.Exp, [dtr], [dtr])
        ACT(v(dt), v(dtr), AF.Ln, [dtr], [dt], bias=1.0)
        TT("dve", v(dtA), v(dt), v(A_b), ALU.mult, [dt, A_b], [dtA])
        for d_ in dirs:
            tri = tri_f if d_ == 0 else tri_r
            MM(pS2[:, 0 + d_ * 32: 0 + d_ * 32 + 16], tri[:], dtA[:, d_, 0:16], True, True, [tri, dtA], [dS2])
            MM(pS2[:, 64 + d_ * 32: 64 + d_ * 32 + 16], onesf[:], dtA[:, d_, 0:16], True, True, [onesf, dtA], [dS2])
        ACT(v(ea), pv(0), AF.Exp, [dS2], [ea])
        ACT(v(cd), pv(64), AF.Exp, [dS2], [cd])
        ACT(v(nac), pv(0), AF.Copy, [dS2], [nac], scale=-1.0)
        TT("dve", v(tmpd), pv(64), v(nac), ALU.add, [dS2, nac], [tmpd])
        ACT(v(dte), v(tmpd), AF.Exp, [tmpd], [dte])
        TT("dve", v(wgt), v(dt), v(dte), ALU.mult, [dt, dte], [wgt])

    pS2 = pS.h
    dS2 = pS.d

    W0D = [halves[0][2], halves[1][2]]
    W1D = [halves[2][2], halves[3][2]]

    def state_update(d_, Sps, Sdeps):
        cd = st_["cd"]
        TT("dve", H[:].rearrange("p (h q) -> p h q", h=16), H[:].rearrange("p (h q) -> p h q", h=16),
           cd[:, d_, 0:16].unsqueeze(2).to_broadcast([128, 16, 64]), ALU.mult, [H, cd], [H])
        TT("dve", H[:], H[:], Sps[:], ALU.add, [H] + Sdeps, [H])

    def local_state(d_, Sps, Sdeps, par):
        wgt = st_["wgt"]
        TT("dve", xw[:].rearrange("p (h q) -> p h q", h=16), xs_tm[par][:].rearrange("p (h q) -> p h q", h=16),
           wgt[:, d_, 0:16].unsqueeze(2).to_broadcast([128, 16, 64]), ALU.mult, [xs_tm[par], wgt], [xw])
        for g in range(4):
            MM(Sps[:, g * 256:(g + 1) * 256], B_tm[par][:, g * 128:(g + 1) * 128], xw[:, g * 256:(g + 1) * 256], True, True, [B_tm[par], xw], Sdeps)

    def h_init(seg, d_):
        if seg["kind"] == "prompt":
            MSET("pool", H[:], 0.0, [H])
        else:
            for half in range(2):
                DMA("sp", stl[:, 0:512].rearrange("p (a b) -> p a b", a=4), st_d[d_, half * 512:(half + 1) * 512, :].rearrange("(a p) n -> p a n", p=128),
                    "ldst", [], [stl])
                for a in range(4):
                    emit("pe", lambda e, a=a: e.transpose(out=pY[:, a * 128:(a + 1) * 128], in_=stl[:, a * 128:(a + 1) * 128], identity=identf[:]),
                         deps([stl, identf]), deps([pY]))
                CP("act", H[:, half * 512:(half + 1) * 512], pY[:, 0:512], [pY], [H])

    def h_out(seq, d_):
        for half in range(2):
            for a in range(4):
                c0 = half * 512 + a * 128
                emit("pe", lambda e, a=a, c0=c0: e.transpose(out=pY[:, a * 128:(a + 1) * 128], in_=H[:, c0:c0 + 128], identity=identf[:]),
                     deps([H, identf]), deps([pY]))
            CP("act", stl[:, 0:512], pY[:, 0:512], [pY], [stl])
            DMA("sp", nst_d[seq, d_, half * 512:(half + 1) * 512, :].rearrange("(a p) n -> p a n", p=128),
                stl[:, 0:512].rearrange("p (a b) -> p a b", a=4), "st_nst", [stl], [])

    segs = [
        dict(kind="prompt", seq=0, ci=0, nch=2, own=2, gc0=0, xsrc=lambda c: xp_d[c * 128:(c + 1) * 128, :], pos=False),
        dict(kind="prompt", seq=1, ci=0, nch=2, own=2, gc0=2, xsrc=lambda c: xp_d[256 + c * 128:256 + (c + 1) * 128, :], pos=False),
        dict(kind="sample", seq=None, ci=1, nch=32, own=16, gc0=4, xsrc=lambda c: xs_d[c * 128:(c + 1) * 128, :], pos=True),
    ]

    slot_ctr = [0]
    xt_ctr = [0]
    B_BIAS = 0.15

    def run_pass(seg, order, be_list, PROJ, BEB, CM=None, FIN=None):
        fwd = order[0] < order[-1] if len(order) > 1 else True
        n = len(order)
        loads = {}
        slots = {}

        def issue_load(i):
            if 0 <= i < n:
                xtile = xt[xt_ctr[0] % 2]
                xt_ctr[0] += 1
                pl = load_x(xtile, seg["xsrc"](order[i]), order[i] if seg["pos"] else None)
                loads[i] = (xtile, pl)

        def fe_thread(i):
            xtile, pl = loads.pop(i)
            sl = slot_ctr[0] % 3
            slot_ctr[0] += 1
            slots[i] = sl
            FE(xtile, pl, sl)
            lo_in, lo_out = (0, 130) if fwd else (130, 0)
            src_prev, src_cur = (128, 2) if fwd else (2, 128)
            seq_first = (order[i] == 0) if fwd else (order[i] == seg["nch"] - 1)
            if i == 0:
                assert seq_first, "pass must start at a sequence end"
                halo_zero(sl, lo_in)
            else:
                halo_copy(sl, lo_in, slots[i - 1], src_prev)
                halo_copy(slots[i - 1], lo_out, sl, src_cur)
            seq_last = (order[i] == seg["nch"] - 1) if fwd else (order[i] == 0)
            if seq_last:
                halo_zero(sl, lo_out)

        issue_load(0)
        for s in range(-3, n):
            issue_load(s + 4)
            threads = []
            if 0 <= s + 3 < n:
                threads.append(record(lambda: fe_thread(s + 3)))
            do_proj = 0 <= s + 1 < n and order[s + 1] in be_list
            do_b = 0 <= s < n and order[s] in be_list

            def t2():
                if do_proj:
                    PROJ(seg, order[s + 1], slots[s + 1], (s + 1) % 2)
                if do_b and CM is not None:
                    CM(seg, order[s], s % 2)
            biases = [0.0] * len(threads)
            threads.append(record(t2))
            biases.append(0.0)
            if do_b:
                threads.append(record(lambda: BEB(seg, order[s], s % 2)))
                biases.append(B_BIAS)
            merge(threads, biases)
            if do_b and FIN is not None:
                FIN(seg, order[s])

    hbsave_ctr = [0]

    def PROJ_light(seg, c, slot, par):
        raw_and_conv(slot, list(range(12)), par)
        dt_raw(slot, [1], par)
        if c < seg["own"]:
            DMA("sp", xsscr[seg["gc0"] + c], xs_tm[par][:], f"st_{xs_tm[par].d.name}", [xs_tm[par]], [])
            DMA("sp", bscr[seg["gc0"] + c], B_tm[par][:], f"st_{B_tm[par].d.name}", [B_tm[par]], [])

    def B_light(seg, c, par):
        dt_stuff([1], par)
        if c < seg["own"]:
            hbt = Hbf[hbsave_ctr[0] % 2]
            hbsave_ctr[0] += 1
            CP("pool", hbt[:], H[:], [H], [hbt])
            DMA("sp", hbscr[seg["gc0"] + c], hbt[:], f"st_{hbt.d.name}", [hbt], [])
        local_state(1, pY, [pY.d], par)
        state_update(1, pY, [pY.d])

    cur_ci = [None]

    def ensure_mod1(ci):
        if cur_ci[0] != ci:
            load_mod1(ci)
            cur_ci[0] = ci

    for seg in segs:
        ensure_mod1(seg["ci"])
        h_init(seg, 1)
        order = list(range(seg["nch"] - 1, -1, -1))
        run_pass(seg, order, order, PROJ_light, B_light)
        if seg["kind"] == "prompt":
            h_out(seg["seq"], 1)
    hb_toks = [(f"st_{n_}{i}", S.dma_count[f"st_{n_}{i}"], "dma") for i in range(2) for n_ in ("Hbf", "xs_tm", "B_tm")]

    ssdg_b = sbt("ssdg_b", [128, D], BF16)
    lng_b = sbt("lng_b", [128, D], BF16)
    lnb_b = sbt("lnb_b", [128, D], BF16)
    bs_t = sbt("bs_t", [128, 8])
    wsT = sbt("wsT", [128, 1024], BF16)
    DMA("pool", ssdg_b[:], vecs_d[3:4, :].partition_broadcast(128), "c_ssdg_b", [], [ssdg_b])
    DMA("pool", lng_b[:], vecs_d[4:5, :].partition_broadcast(128), "c_lng_b", [], [lng_b])
    DMA("pool", lnb_b[:], vecs_d[5:6, :].partition_broadcast(128), "c_lnb_b", [], [lnb_b])
    DMA("sp", bs_t[:], bs_d, "c_bs_t", [], [bs_t])
    DMA("pool", wsT[:], wsT_d, "c_wsT", [], [wsT])
    sz = [sbt(f"sz{i}", [128, D], BF16) for i in range(2)]
    gu = [sbt(f"gu{i}", [128, D], BF16) for i in range(2)]
    gv = [sbt(f"gv{i}", [128, D], BF16) for i in range(2)]
    BCT = [sbt(f"BCT{i}", [128, 8, 128], BF16) for i in range(2)]
    cbT = sbt("cbT", [128, 4, 128], BF16)
    ACs = sbt("ACs", [128, 2, 128], BF16)
    NACs = sbt("NACs", [128, 2, 128], BF16)
    tmpb = sbt("tmpb", [128, 2, 128], BF16)
    dtA2 = sbt("dtA2", [128, 2, 128])
    for t_ in (ACs, NACs, dtA2):
        MSET("pool", t_[:], 0.0, [t_])
    decT = [sbt(f"decT{i}", [128, 4, 128], BF16) for i in range(2)]
    MT = [sbt(f"MT{i}", [128, 4, 128], BF16) for i in range(2)]
    cat = [sbt("cat0", [128, 2048], BF16)]
    S.wait_all("sp", hb_toks)

    def PROJ_full(seg, c, slot, par):
        hx = hTx[slot]
        gc_ = seg["gc0"] + c
        DMA("sp", xs_tm[par][:], xsscr[gc_], f"ld_{xs_tm[par].d.name}", [], [xs_tm[par]])
        DMA("sp", B_tm[par][:], bscr[gc_], f"ld_{B_tm[par].d.name}", [], [B_tm[par]])
        raw_only(slot, [12, 13, 14, 15])
        pt_, off, dd = next_half(0)
        for g in range(4):
            ft = 12 + g
            for k in range(5):
                MM(pt_[:, off + g * 128: off + (g + 1) * 128], convdiag[:, ft * 5 + k, :], rawT[:, ft, k:k + 128], k == 0, k == 4, [rawT, convdiag], [dd])
        for g in range(4):
            ACT(BCT[par][:, 4 + g, :], pt_[:, off + g * 128: off + (g + 1) * 128], AF.Silu, [dd, convbT], [BCT[par]], bias=convbT[:, 12 + g:13 + g])
        pt_, off, dd = next_half(0)
        pb = pt_[:, off:off + 256].bitcast(BF16)
        for g in range(4):
            TR(pb[:, g * 128:(g + 1) * 128], B_tm[par][:, g * 128:(g + 1) * 128], ident[:], [B_tm[par], ident], [dd])
        CP("act", BCT[par][:, 0:4, :], pb.rearrange("p (a b) -> p a b", a=4), [dd], [BCT[par]])
        for (c0, dst, fn) in ((C_Z, sz[par], AF.Silu), (C_U, gu[par], AF.Gelu_apprx_tanh), (C_V, gv[par], AF.Gelu_apprx_tanh)):
            for nb in range(2):
                pt_, off, dd = next_half(0)
                for kt in range(8):
                    MM(pt_[:, off:off + 512], hx[:, kt, 2:130], Win[:, kt, c0 + nb * 512: c0 + (nb + 1) * 512], kt == 0, kt == 7, [hx, Win], [dd])
                ACT(dst[:, nb * 512:(nb + 1) * 512], pt_[:, off:off + 512], fn, [dd], [dst])
        dt_raw(slot, [0, 1], par)

    cat_a, cat_b = Dep("cat_a"), Dep("cat_b")
    vnb = sbt("vnb", [128, D], BF16)
    cmt = sbt("cmt", [128, D], BF16)
    ssC = sbt("ssC", [128, 4])
    bnst = sbt("bnst2", [128, 2, 6])
    mv = sbt("mv2", [128, 2])

    def CM_full(seg, c, par):
        gu_, gv_ = gu[par], gv[par]
        ct = cat[0]
        for q in range(2):
            emit("dve", lambda e, q=q: e.bn_stats(out=bnst[:, q, :], in_=gv_[:, q * 512:(q + 1) * 512]), deps([gv_]), deps([bnst]))
        emit("dve", lambda e: e.bn_aggr(out=mv[:], in_=bnst[:]), deps([bnst]), deps([mv]))
        ACT(ssC[:, 0:1], mv[:, 1:2], AF.Ln, [mv], [ssC], bias=eps_t[:, 0:1])
        ACT(ssC[:, 0:1], ssC[:, 0:1], AF.Exp, [ssC], [ssC], scale=-0.5)
        TS("dve", cmt[:], gv_[:], mv[:, 0:1], ssC[:, 0:1], ALU.subtract, ALU.mult, [gv_, mv, ssC], [cmt])
        TT("dve", cmt[:], cmt[:], lng_b[:], ALU.mult, [cmt, lng_b], [cmt])
        TT("dve", vnb[:], cmt[:], lnb_b[:], ALU.add, [cmt, lnb_b], [vnb])
        for half in range(2):
            pt_, off, dd = next_half(0)
            for hh in range(4):
                h_ = half * 4 + hh
                MM(pt_[:, off + hh * 128: off + (hh + 1) * 128], wsT[:, h_ * 128:(h_ + 1) * 128], vnb[:, h_ * 128:(h_ + 1) * 128], True, True, [wsT, vnb], [dd])
            TT("dve", cmt[:, half * 512:(half + 1) * 512].rearrange("p (h q) -> p h q", h=4), pt_[:, off:off + 512].rearrange("p (h q) -> p h q", h=4),
               bs_t[:, half * 4:(half + 1) * 4].unsqueeze(2).to_broadcast([128, 4, 128]), ALU.add, [dd, bs_t], [cmt])
        TT("dve", ct[:, 1024:2048], cmt[:], gu_[:], ALU.mult, [cmt, gu_], [cat_b])

    def FIN_full(seg, c):
        gc = seg["gc0"] + c
        DMA("sp", catscr[gc * 128:(gc + 1) * 128, :], cat[0][:], "catst", [cat_a, cat_b], [])

    def B_full(seg, c, par):
        gc = seg["gc0"] + c
        xs_, B_, BCT_, sz_ = xs_tm[par], B_tm[par], BCT[par], sz[par]
        DMA("sp", Hst1[:], hbscr[gc], "ldhb", [], [Hst1])
        dt_stuff([0, 1], par)
        dtA, nac, ea, dt, cd, wgt = st_["dtA"], st_["nac"], st_["ea"], st_["dt"], st_["cd"], st_["wgt"]
        CP("dve", Hbf[0][:], H[:], [H], [Hbf[0]])
        CP("dve", dtA2[:, :, 0:64].rearrange("p d (r c) -> p d r c", r=2), dtA[:].unsqueeze(2).to_broadcast([128, 2, 2, 32]), [dtA], [dtA2])
        for d_ in range(2):
            tri = tri_f if d_ == 0 else tri_r
            MM(pS2[:, 128 + d_ * 128: 128 + (d_ + 1) * 128], dtA2[:, d_, :], tri[:], True, True, [dtA2, tri], [dS2])
        acv = pS2[:, 128:384].rearrange("p (a b) -> p a b", a=2)
        CP("act", ACs[0:32], acv[0:32], [dS2], [ACs])
        CP("act", tmpb[32:64], acv[32:64], [dS2], [tmpb])
        TT("dve", ACs[32:64], acv[32:64], tmpb[32:64], ALU.subtract, [dS2, tmpb], [ACs])
        TS("dve", NACs[0:64], ACs[0:64], -1.0, None, ALU.mult, None, [ACs], [NACs])
        pt_, off, dd = next_half(1)
        for g in range(4):
            MM(pt_[:, off + g * 128: off + (g + 1) * 128], BCT_[:, g, :], BCT_[:, 4 + g, :], True, True, [BCT_], [dd])
        CP("act", cbT[:], pt_[:, off:off + 512].rearrange("p (a b) -> p a b", a=4), [dd], [cbT])

        grp = {}

        def dec_mm(g, d_):
            pt_, off, dd = next_half(1)
            wide = pt_[:, off:off + 512].rearrange("p (a b) -> p a b", a=4)
            MM(wide, NACs[:, d_, :], SEL2[:, g * 4:(g + 1) * 4, :], True, False, [SEL2, NACs], [dd])
            MM(wide, ident[:], negm[:, d_, :].unsqueeze(1).to_broadcast([128, 4, 128]), False, False, [ident, negm], [dd])
            for hh in range(4):
                h_ = g * 4 + hh
                o = pt_[:, off + hh * 128: off + (hh + 1) * 128]
                MM(o, SEL2[:, h_, :], ACs[:, d_, :], False, hh == 3, [SEL2, ACs], [dd])
            grp[(g, d_)] = (pt_, off, dd)

        def dec_ev(g, d_):
            pt_, off, dd = grp.pop((g, d_))
            ACT(decT[d_][:], pt_[:, off:off + 512].rearrange("p (a b) -> p a b", a=4), AF.Exp, [dd], [decT[d_]])
            TT("dve", MT[d_][:], decT[d_][:], cbT[:, g, :].unsqueeze(1).to_broadcast([128, 4, 128]), ALU.mult, [decT[d_], cbT], [MT[d_]])

        def y_group(g):
            for hh in range(4):
                h_ = g * 4 + hh
                for d_ in range(2):
                    MM(pY[:, h_ * 64:(h_ + 1) * 64], MT[d_][:, hh, :], xdt[d_][:, h_ * 64:(h_ + 1) * 64], d_ == 0, d_ == 1, [MT[d_], xdt[d_]], [pY])

        dec_mm(0, 0)
        dec_mm(0, 1)
        for d_ in range(2):
            TT("pool", xdt[d_][:].rearrange("p (h q) -> p h q", h=16), xs_[:].rearrange("p (h q) -> p h q", h=16),
               dt[:, d_, 0:16].unsqueeze(2).to_broadcast([128, 16, 64]), ALU.mult, [xs_, dt], [xdt[d_]])
        v3h = lambda ap: ap.rearrange("p (h q) -> p h q", h=8)
        bch = lambda t, d_, half: t[:, d_, half * 8:(half + 1) * 8].unsqueeze(2).to_broadcast([128, 8, 64])
        for half in range(2):
            hs = slice(half * 512, (half + 1) * 512)
            for d_, Hs in enumerate((Hbf[0], Hst1)):
                for gg in range(2):
                    g = half * 2 + gg
                    MM(pS2[:, gg * 256:(gg + 1) * 256], BCT_[:, 4 + g, :], Hs[:, g * 256:(g + 1) * 256], True, True, [BCT_, Hs], [dS2])
                if d_ == 0:
                    TT("dve", v3h(ysb[:, hs]), v3h(pS2[:, 0:512]), bch(ea, 0, half), ALU.mult, [dS2, ea], [ysb])
                else:
                    TT("dve", v3h(t1[:, hs]), v3h(pS2[:, 0:512]), bch(ea, 1, half), ALU.mult, [dS2, ea], [t1])
                    TT("dve", ysb[:, hs], ysb[:, hs], t1[:, hs], ALU.add, [ysb, t1], [ysb])
        TT("pool", t1[:].rearrange("p (h q) -> p h q", h=16), xs_[:].rearrange("p (h q) -> p h q", h=16),
           dskip_b[:].unsqueeze(2).to_broadcast([128, 16, 64]), ALU.mult, [xs_, dskip_b], [t1])
        TT("dve", ysb[:], ysb[:], t1[:], ALU.add, [ysb, t1], [ysb])
        xw_ = Hbf[0]
        TT("dve", xw_[:].rearrange("p (h q) -> p h q", h=16), xs_[:].rearrange("p (h q) -> p h q", h=16),
           wgt[:, 0, 0:16].unsqueeze(2).to_broadcast([128, 16, 64]), ALU.mult, [xs_, wgt], [xw_])
        for half in range(2):
            hs = slice(half * 512, (half + 1) * 512)
            for gg in range(2):
                g = half * 2 + gg
                MM(pS2[:, gg * 256:(gg + 1) * 256], B_[:, g * 128:(g + 1) * 128], xw_[:, g * 256:(g + 1) * 256], True, True, [B_, xw_], [dS2])
            TT("dve", v3h(H[:, hs]), v3h(H[:, hs]), bch(cd, 0, half), ALU.mult, [H, cd], [H])
            TT("dve", H[:, hs], H[:, hs], pS2[:, 0:512], ALU.add, [H, dS2], [H])
        for g in range(4):
            dec_ev(g, 0)
            if g + 1 < 4:
                dec_mm(g + 1, 0)
            dec_ev(g, 1)
            if g + 1 < 4:
                dec_mm(g + 1, 1)
            y_group(g)
        TT("dve", ysb[:], ysb[:], pY[:], ALU.add, [ysb, pY], [ysb])
        TT("dve", ysb[:], ysb[:], sz_[:], ALU.mult, [ysb, sz_], [ysb])
        rms_rstd(ysb[:], ysb, ssB, 1, t1[:], t1)
        STT(cat[0][:, 0:1024], ysb[:], ssB[:, 1:2], ssdg_b[:], ALU.mult, ALU.mult, [ysb, ssB, ssdg_b], [cat_a])

    for seg in segs:
        ensure_mod1(seg["ci"])
        h_init(seg, 0)
        own = seg["own"]
        nfe = own if seg["kind"] == "prompt" else own + 1
        run_pass(seg, list(range(nfe)), list(range(own)), PROJ_full, B_full, CM_full, FIN_full)
        if seg["kind"] == "prompt":
            h_out(seg["seq"], 0)
    cat_done = [("catst", S.dma_count["catst"], "dma")]

    def barrier_tokens():
        toks = [(e_, S.count[e_], e_) for e_ in S.ENGS if S.count[e_] > 0]
        toks += [(k_, v_, "dma") for k_, v_ in S.dma_count.items()]
        return toks

    def guard(tiles, toks):
        for t_ in tiles:
            t_.d.readers.extend(toks)

    bar1 = barrier_tokens()
    A.reset(m_core)
    Wout = sbt("Wout", [128, 16, D], BF16)
    g1b = [sbt(f"g1b{i}", [128, D]) for i in range(2)]
    xt2 = [sbt(f"x2t{i}", [128, D]) for i in range(2)]
    posL2 = [sbt(f"posL2{i}", [128, 512]) for i in range(2)]
    catl = [sbt(f"catl{i}", [128, 2048], BF16) for i in range(2)]
    catT = sbt("catT", [128, 16, 128], BF16)
    x1t = [sbt(f"x1t{i}", [128, D]) for i in range(2)]
    W_BASE = Arena.END - 131072
    assert A.cur <= W_BASE, f"stage2 working set overlaps FFN weights: {A.cur} > {W_BASE}"
    A.cur = W_BASE
    W1 = sbt("W1", [128, 8, 4096], BF16)
    W2 = sbt("W2", [128, 32, D], BF16)
    guard([Wout] + g1b + xt2 + posL2 + catl + [catT] + x1t + [W1, W2], bar1)
    wout_v = wout_d.rearrange("(kt p) n -> p kt n", p=128)
    w1_v = wff1_d.rearrange("(kt p) n -> p kt n", p=128)
    w2_v = wff2_d.rearrange("(kt p) n -> p kt n", p=128)
    dWout = [Dep(f"Wout{i}") for i in range(4)]
    dW1 = [Dep(f"W1_{i}") for i in range(8)]
    dW2 = [Dep(f"W2_{i}") for i in range(8)]
    for d_ in dWout + dW1 + dW2:
        d_.readers.extend(bar1)
    for kt4 in range(4):
        DMA("pool", Wout[:, kt4 * 4:(kt4 + 1) * 4, :], wout_v[:, kt4 * 4:(kt4 + 1) * 4, :], f"w_out{kt4}", [], [dWout[kt4]])
    S.wait_all("pool", [(f"w_out{i}", 16, "dma") for i in range(4)])
    for kt in range(8):
        DMA("pool", W1[:, kt, :], w1_v[:, kt, :], f"w_ff1_{kt}", [], [dW1[kt]])
    for k4 in range(8):
        DMA("pool", W2[:, k4 * 4:(k4 + 1) * 4, :], w2_v[:, k4 * 4:(k4 + 1) * 4, :], f"w_ff2_{k4}", [], [dW2[k4]])
    for ci in range(2):
        DMA("sp", g1b[ci][:], modscr[ci, :, 2048:3072], f"c_g1b{ci}", [], [g1b[ci]])
    S.wait_all("sp", cat_done)
    xt = xt2
    posL = posL2

    def tile_src(gc):
        if gc < 4:
            return xp_d[gc * 128:(gc + 1) * 128, :], None, 0
        c = gc - 4
        return xs_d[c * 128:(c + 1) * 128, :], c, 1

    pend = {}

    def s2_load(gc):
        if gc < 20:
            src, pc, ci = tile_src(gc)
            xtile = xt[gc % 2]
            pl = load_x(xtile, src, pc)
            ctl = catl[gc % 2]
            DMA("sp", ctl[:], catscr[gc * 128:(gc + 1) * 128, :], f"ld_{ctl.d.name}", [], [ctl])
            pend[gc] = (xtile, pl, ctl, ci)

    s2_load(0)
    for gc in range(20):
        s2_load(gc + 1)
        xtile, pl, ctl, ci = pend.pop(gc)
        add_pos(xtile, pl)
        for half in range(2):
            for a in range(8):
                kt = half * 8 + a
                TR(pT[:, a * 128:(a + 1) * 128], ctl[:, kt * 128:(kt + 1) * 128], ident[:], [ctl, ident], [pT])
            CP("act", catT[:, half * 8:(half + 1) * 8, :], pT[:].rearrange("p (a b) -> p a b", a=8), [pT], [catT])
        pw = pW0 if gc % 2 == 0 else pW1
        hd_ = W0D if gc % 2 == 0 else W1D
        for nb in range(2):
            for kt in range(16):
                MM(pw[:, nb * 512:(nb + 1) * 512], catT[:, kt, :], Wout[:, kt, nb * 512:(nb + 1) * 512], kt == 0, kt == 15, [catT, dWout[kt // 4]], [hd_[nb]])
        x1 = x1t[gc % 2]
        TT("dve", x1[:], pw[:], g1b[ci][:], ALU.mult, hd_ + [g1b[ci]], [x1])
        TT("pool", x1[:], x1[:], xtile[:], ALU.add, [x1, xtile], [x1])
        DMA("sp", x1scr[gc * 128:(gc + 1) * 128, :], x1[:], f"st_{x1.d.name}", [x1], [])
    x1_done = [(f"st_x1t{i}", S.dma_count[f"st_x1t{i}"], "dma") for i in range(2)]

    bar2 = barrier_tokens()
    A.reset(m_core)
    G2 = sbt("G2", [128, D])
    sh2 = sbt("sh2", [128, D])
    g2b = sbt("g2b", [128, D])
    fg_b = sbt("fg_b", [128, D])
    x1l = [sbt(f"x1l{i}", [128, 2, D]) for i in range(2)]
    h2 = sbt("h2", [128, D], BF16)
    h2T = sbt("h2T", [128, 8, 256], BF16)
    aT = sbt("aT", [128, 32, 256], BF16)
    rl = [sbt(f"rl{i}", [128, 256], BF16) for i in range(2)]
    yo = [sbt(f"yo{i}", [128, D]) for i in range(2)]
    junk3 = sbt("junk3", [128, D], BF16)
    ss3 = sbt("ss3", [128, 4])
    tmp3 = sbt("tmp3", [128, D])
    assert A.cur <= W_BASE, f"stage3 working set overlaps FFN weights: {A.cur} > {W_BASE}"
    guard([G2, sh2, g2b, fg_b] + x1l + [h2, h2T, aT] + rl + yo + [junk3, ss3, tmp3], bar2)
    DMA("sp", fg_b[:], vecs_d[2:3, :].partition_broadcast(128), "c_fg_b", [], [fg_b])
    S.wait_all("sp", x1_done)

    h2Tb = [h2T, sbt("h2T1", [128, 8, 256], BF16)]
    ss3p = sbt("ss3p", [128, 4])
    guard([h2Tb[1], ss3p], bar2)
    assert A.cur <= W_BASE, f"stage3 working set overlaps FFN weights: {A.cur} > {W_BASE}"

    def load_G2sh2(ci):
        DMA("sp", tmp3[:], vecs_d[1:2, :].partition_broadcast(128), "c_tmp3", [], [tmp3])
        DMA("sp", sh2[:], modscr[ci, :, 3072:4096], "c_sh2", [], [sh2])
        DMA("sp", G2[:], modscr[ci, :, 4096:5120], "c_G2", [], [G2])
        STT(G2[:], G2[:], 1.0, tmp3[:], ALU.add, ALU.mult, [G2, tmp3], [G2])

    def load_g2b(ci):
        DMA("sp", g2b[:], modscr[ci, :, 5120:6144], "c_g2b", [], [g2b])

    def rms3(src_ap, src_t, sst, col, dump_ap, dump_t):
        ACT(dump_ap, src_ap, AF.Square, [src_t], [dump_t, sst], accum_out=sst[:, col:col + 1])
        ACT(sst[:, col:col + 1], sst[:, col:col + 1], AF.Ln, [sst], [sst], scale=1.0 / D, bias=eps_t[:, 0:1])
        ACT(sst[:, col:col + 1], sst[:, col:col + 1], AF.Exp, [sst], [sst], scale=-0.5)

    def s3_load(b):
        if b < 10:
            xl = x1l[b % 2]
            DMA("sp", xl[:], x1scr[b * 256:(b + 1) * 256, :].rearrange("(a p) n -> p a n", p=128), f"ld_{xl.d.name}", [], [xl])

    def prep(b):
        xl = x1l[b % 2]
        hT = h2Tb[b % 2]
        for t_i in range(2):
            rms3(xl[:, t_i, :], xl, ss3p, 0, h2[:], h2)
            STT(h2[:], xl[:, t_i, :], ss3p[:, 0:1], G2[:], ALU.mult, ALU.mult, [xl, ss3p, G2], [h2])
            TT("dve", h2[:], h2[:], sh2[:], ALU.add, [h2, sh2], [h2])
            for kt in range(8):
                TR(pT[:, kt * 128:(kt + 1) * 128], h2[:, kt * 128:(kt + 1) * 128], ident[:], [h2, ident], [pT])
            CP("act", hT[:, :, t_i * 128:(t_i + 1) * 128], pT[:].rearrange("p (a b) -> p a b", a=8), [pT], [hT])

    yo_ctr = [0]

    def ffn(b):
        xl = x1l[b % 2]
        hT = h2Tb[b % 2]
        for ft in range(32):
            pt_, off, dd = halves[2 + ft % 2]
            for kt in range(8):
                MM(pt_[:, off:off + 256], W1[:, kt, ft * 128:(ft + 1) * 128], hT[:, kt, :], kt == 0, kt == 7, [dW1[kt], hT], [dd])
            r_ = rl[ft % 2]
            ACT(r_[:], pt_[:, off:off + 256], AF.Relu, [dd], [r_])
            TT("dve", aT[:, ft, :], r_[:], r_[:], ALU.mult, [r_], [aT])
        for t_i in range(2):
            for nb in range(2):
                dd = W0D[nb] if t_i == 0 else pY.d
                o = pW0[:, nb * 512:(nb + 1) * 512] if t_i == 0 else pY[:, nb * 512:(nb + 1) * 512]
                for ft in range(32):
                    MM(o, aT[:, ft, t_i * 128:(t_i + 1) * 128], W2[:, ft, nb * 512:(nb + 1) * 512], ft == 0, ft == 31, [aT, dW2[ft // 4]], [dd])
            src = pW0 if t_i == 0 else pY
            rd = W0D if t_i == 0 else [pY.d]
            y_ = yo[yo_ctr[0] % 2]
            yo_ctr[0] += 1
            TT("dve", tmp3[:], src[:], g2b[:], ALU.mult, rd + [g2b], [tmp3])
            TT("dve", tmp3[:], tmp3[:], xl[:, t_i, :], ALU.add, [tmp3, xl], [tmp3])
            rms3(tmp3[:], tmp3, ss3, 1, junk3[:], junk3)
            STT(y_[:], tmp3[:], ss3[:, 1:2], fg_b[:], ALU.mult, ALU.mult, [tmp3, ss3, fg_b], [y_])
            gc = b * 2 + t_i
            if gc < 4:
                dst = yp_d[gc * 128:(gc + 1) * 128, :]
            else:
                dst = ys_d[(gc - 4) * 128:(gc - 3) * 128, :]
            DMA("sp", dst, y_[:], f"st_{y_.d.name}", [y_], [])

    s3_load(0)
    s3_load(1)
    load_G2sh2(0)
    load_g2b(0)
    prep(0)
    for b in range(10):
        if b + 1 == 2:
            load_G2sh2(1)
        ths = [record(lambda: ffn(b))]
        if b + 1 < 10:
            ths.append(record(lambda: prep(b + 1)))
        merge(ths, [0.0, 0.25])
        if b + 1 == 2:
            load_g2b(1)
        s3_load(b + 2)

    S.wait_all("sp", [(k_, v_, "dma") for k_, v_ in S.dma_count.items()])
    S.emit()
    return nc


_NC_CACHE = {}


def _prep_inputs(inp):
    f = lambda a: np.ascontiguousarray(np.asarray(a, dtype=np.float32))
    x_prompt, x_sample, state = f(inp["x_prompt"]), f(inp["x_sample"]), f(inp["state_ssd"])
    c, c_ctx = f(inp["c"]), f(inp["c_ctx"])
    w_in = f(inp["w_in"])[0]
    conv_w, conv_b = f(inp["conv_w"])[0], f(inp["conv_b"])[0]
    dt_bias, A_log = f(inp["dt_bias"])[0], f(inp["A_log"])[0]
    cm_w_s, cm_b_s = f(inp["cm_w_s"])[0], f(inp["cm_b_s"])[0]
    vecs = np.stack([f(inp["norm1_g"])[0], f(inp["norm2_g"])[0], f(inp["final_norm_g"]), f(inp["ssd_norm_g"])[0],
                     f(inp["cm_ln_g"])[0], f(inp["cm_ln_b"])[0]], axis=0)
    shared = {
        "w_ada": f(inp["w_ada"])[0], "b_ada": f(inp["b_ada"])[0][None, :], "vecs": f(vecs),
        "conv_b": conv_b[None, :], "conv_bT": f(conv_b.reshape(16, 128).T), "dskip": f(inp["d_skip"])[0][None, :], "w_out": f(inp["w_out"])[0],
        "w_ff1": f(inp["w_ff1"])[0], "w_ff2": f(inp["w_ff2"])[0],
    }
    variants = []
    for mir in (False, True):
        v = dict(shared)
        if not mir:
            v["w_in"] = w_in
            cw = conv_w
            dtb, alog = dt_bias, A_log
            ws, bsv = cm_w_s, cm_b_s
            v["rowpos"] = np.arange(64, dtype=np.float32)[:, None]
            v["colpos"] = f((np.arange(128) % 64).astype(np.float32)[:, None])
        else:
            wi = w_in.copy()
            wi[:, C_DT:C_DT + 16] = w_in[:, C_DT + 16:C_DT + 32]
            wi[:, C_DT + 16:C_DT + 32] = w_in[:, C_DT:C_DT + 16]
            v["w_in"] = wi
            cw = conv_w[::-1]
            dtb, alog = dt_bias[::-1], A_log[::-1]
            ws, bsv = cm_w_s[:, ::-1, ::-1], cm_b_s[:, ::-1]
            v["rowpos"] = f(np.arange(63, -1, -1, dtype=np.float32)[:, None])
            v["colpos"] = f((63 - (np.arange(128) % 64)).astype(np.float32)[:, None])
        v["conv_wT"] = f(cw.reshape(5, 16, 128).transpose(2, 1, 0).reshape(128, 80))
        v["dtb"] = f(dtb.reshape(1, 32))
        v["alog"] = f(alog.reshape(1, 32))
        v["wsT"] = f(ws.transpose(2, 0, 1).reshape(128, 1024))
        v["bs"] = f(bsv.T)
        variants.append(v)
    in_maps = []
    for k in range(8):
        b, mir = k // 2, (k % 2 == 1)
        m = dict(variants[1 if mir else 0])
        xs = x_sample[b]
        xp = x_prompt[2 * k:2 * k + 2]
        st = state[b, 0]
        if mir:
            xs = xs[::-1]
            xp = xp[:, ::-1]
            st = st[::-1]
        m["xs"] = f(xs)
        m["xp"] = f(xp.reshape(512, D))
        m["st_in"] = f(st.reshape(2, 1024, 128))
        cond = np.stack([c_ctx, c[b]], axis=0)
        m["condT"] = f(cond.reshape(2, 8, 128).transpose(2, 0, 1).reshape(128, 16))
        in_maps.append(m)
    return in_maps


def kernel(**inputs):
    if "nc" not in _NC_CACHE:
        _NC_CACHE["nc"] = build_program()
    nc = _NC_CACHE["nc"]
    in_maps = _prep_inputs(inputs)
    res = run_bass_kernel_spmd(nc, in_maps, core_ids=list(range(8)))
    y_prompt = np.zeros((16, 256, D), np.float32)
    y_sample = np.zeros((4, 4096, D), np.float32)
    new_state = np.zeros((16, 1, 2, 16, 64, 128), np.float32)
    for k in range(8):
        r = res.results[k]
        b, mir = k // 2, (k % 2 == 1)
        yp = np.asarray(r["yp"]).reshape(2, 256, D)
        ys = np.asarray(r["ys"])
        ns = np.asarray(r["nst"]).reshape(2, 2, 16, 64, 128)
        if mir:
            yp = yp[:, ::-1]
            ys = ys[::-1]
            ns = ns[:, ::-1]
            y_sample[b, 2048:] = ys
        else:
            y_sample[b, :2048] = ys
        y_prompt[2 * k:2 * k + 2] = yp
        new_state[2 * k:2 * k + 2, 0] = ns
    return (y_prompt, y_sample, new_state)
```
